# Optimizing a Trainium2 kernel written in Bass

```python
import math
import jax, jax.numpy as jnp
from jax import lax
import numpy as np

D_MODEL = 1024
BATCH = 8
SEQ = 2048
DEPTH = 2

MLA_HEADS = 6
MLA_Q_RANK = 384
MLA_KV_RANK = 256
MLA_NOPE = 64
MLA_ROPE = 32
MLA_QK = MLA_NOPE + MLA_ROPE
MLA_V = 64
DIFF_HEADS = 4
DIFF_QK = 64
DIFF_V = 2 * DIFF_QK
GQA_HEADS = 6
GQA_KV_HEADS = 2
GQA_GROUP = GQA_HEADS // GQA_KV_HEADS
GQA_DIM = 64
GRID_W = 64
Q_BLOCK = 128
ROPE_THETA = 10000.0
EPS = 1e-6
FFN_HIDDEN = -(-(8 * D_MODEL) // (3 * 256)) * 256

IN_SIZES = [
    MLA_Q_RANK, MLA_KV_RANK, MLA_ROPE,
    DIFF_HEADS * 2 * DIFF_QK, DIFF_HEADS * 2 * DIFF_QK, DIFF_HEADS * DIFF_V,
    GQA_HEADS * GQA_DIM, GQA_KV_HEADS * GQA_DIM, GQA_KV_HEADS * GQA_DIM,
]
IN_WIDTH = sum(IN_SIZES)
IN_OFFSETS = [int(v) for v in np.cumsum(IN_SIZES)[:-1]]
MIX_WIDTH = MLA_HEADS * MLA_V + DIFF_HEADS * DIFF_V + GQA_HEADS * GQA_DIM

kernel_name = "hybrid_mla_diff_axialgqa_encoder"


def _rms_norm(x, gain):
    xf = x.astype(jnp.float32)
    y = xf * lax.rsqrt(jnp.mean(xf * xf, axis=-1, keepdims=True) + EPS)
    return (y * gain.astype(jnp.float32)).astype(x.dtype)


def _rope_cos_sin(pos, dim):
    inv = 1.0 / (ROPE_THETA ** (jnp.arange(0, dim, 2, dtype=jnp.float32) / dim))
    ang = pos.astype(jnp.float32)[:, None] * inv[None, :]
    return jnp.cos(ang), jnp.sin(ang)


def _apply_rope(x, cos, sin):
    xf = x.astype(jnp.float32)
    half = x.shape[-1] // 2
    x1, x2 = xf[..., :half], xf[..., half:]
    return jnp.concatenate([x1 * cos - x2 * sin, x1 * sin + x2 * cos], axis=-1).astype(x.dtype)


def _sweep_query_blocks(block_fn, q):
    s_len = q.shape[-2]
    nb = s_len // Q_BLOCK
    qb = q.reshape(q.shape[:-2] + (nb, Q_BLOCK, q.shape[-1]))
    qb = jnp.moveaxis(qb, -3, 0)
    out = lax.map(block_fn, qb)
    out = jnp.moveaxis(out, 0, -3)
    return out.reshape(out.shape[:-3] + (s_len, out.shape[-1]))


def _dense_attention(q, k, v, scale, score_eq, out_eq):
    def block_fn(qb):
        s = jnp.einsum(score_eq, qb, k).astype(jnp.float32) * scale
        p = jax.nn.softmax(s, axis=-1)
        return jnp.einsum(out_eq, p.astype(v.dtype), v)
    return _sweep_query_blocks(block_fn, q)


def _normal(key, shape, scale):
    return jax.random.normal(key, shape, jnp.float32) * scale


def _gain(key, shape):
    return 1.0 + 0.02 * jax.random.normal(key, shape, jnp.float32)


def setup_inputs(seed: int = 0) -> dict:
    key = jax.random.key(seed)
    ks = jax.random.split(key, 24)
    L = DEPTH
    return {
        "x": jax.random.normal(ks[0], (BATCH, SEQ, D_MODEL), jnp.float32),
        "attn_norm": _gain(ks[1], (L, D_MODEL)),
        "w_in": _normal(ks[2], (L, D_MODEL, IN_WIDTH), D_MODEL ** -0.5),
        "mla_q_norm": _gain(ks[3], (L, MLA_Q_RANK)),
        "mla_w_uq": _normal(ks[4], (L, MLA_Q_RANK, MLA_HEADS * MLA_QK), MLA_Q_RANK ** -0.5),
        "mla_kv_norm": _gain(ks[5], (L, MLA_KV_RANK)),
        "mla_w_ukv": _normal(ks[6], (L, MLA_KV_RANK, MLA_HEADS * (MLA_NOPE + MLA_V)), MLA_KV_RANK ** -0.5),
        "mla_q_gain": _gain(ks[7], (L, MLA_QK)),
        "mla_k_gain": _gain(ks[8], (L, MLA_QK)),
        "diff_q_gain": _gain(ks[9], (L, DIFF_QK)),
        "diff_k_gain": _gain(ks[10], (L, DIFF_QK)),
        "diff_lq1": _normal(ks[11], (L, DIFF_QK), 0.1),
        "diff_lk1": _normal(ks[12], (L, DIFF_QK), 0.1),
        "diff_lq2": _normal(ks[13], (L, DIFF_QK), 0.1),
        "diff_lk2": _normal(ks[14], (L, DIFF_QK), 0.1),
        "diff_out_gain": _gain(ks[15], (L, DIFF_V)),
        "gqa_q_gain": _gain(ks[16], (L, GQA_DIM)),
        "gqa_k_gain": _gain(ks[17], (L, GQA_DIM)),
        "w_o": _normal(ks[18], (L, MIX_WIDTH, D_MODEL), MIX_WIDTH ** -0.5),
        "ffn_norm": _gain(ks[19], (L, D_MODEL)),
        "w_gate_up": _normal(ks[20], (L, D_MODEL, 2 * FFN_HIDDEN), D_MODEL ** -0.5),
        "w_down": _normal(ks[21], (L, FFN_HIDDEN, D_MODEL), FFN_HIDDEN ** -0.5),
    }


def reference(x, attn_norm, w_in, mla_q_norm, mla_w_uq, mla_kv_norm, mla_w_ukv,
              mla_q_gain, mla_k_gain, diff_q_gain, diff_k_gain, diff_lq1, diff_lk1,
              diff_lq2, diff_lk2, diff_out_gain, gqa_q_gain, gqa_k_gain, w_o,
              ffn_norm, w_gate_up, w_down):
    B, S, _ = x.shape
    rows = S // GRID_W
    pos = jnp.arange(S, dtype=jnp.int32)
    cos_mla, sin_mla = _rope_cos_sin(pos, MLA_ROPE)
    cos_dif, sin_dif = _rope_cos_sin(pos, DIFF_QK)
    row_idx, col_idx = jnp.meshgrid(jnp.arange(rows, dtype=jnp.int32),
                                    jnp.arange(GRID_W, dtype=jnp.int32), indexing="ij")
    half_ax = GQA_DIM // 2
    cos_row, sin_row = _rope_cos_sin(row_idx.reshape(-1), half_ax)
    cos_col, sin_col = _rope_cos_sin(col_idx.reshape(-1), half_ax)

    def axial_rope(t):
        return jnp.concatenate([_apply_rope(t[..., :half_ax], cos_row, sin_row),
                                _apply_rope(t[..., half_ax:], cos_col, sin_col)], axis=-1)

    for l in range(DEPTH):
        xn = _rms_norm(x, attn_norm[l])
        h = xn @ w_in[l]
        (h_cq, h_ckv, h_kr, h_dq, h_dk, h_dv, h_gq, h_gk, h_gv) = jnp.split(h, IN_OFFSETS, axis=-1)

        c_q = _rms_norm(h_cq, mla_q_norm[l])
        q_a = (c_q @ mla_w_uq[l]).reshape(B, S, MLA_HEADS, MLA_QK).transpose(0, 2, 1, 3)
        c_kv = _rms_norm(h_ckv, mla_kv_norm[l])
        kv_a = (c_kv @ mla_w_ukv[l]).reshape(B, S, MLA_HEADS, MLA_NOPE + MLA_V).transpose(0, 2, 1, 3)
        k_nope, v_a = kv_a[..., :MLA_NOPE], kv_a[..., MLA_NOPE:]
        k_rope = jnp.broadcast_to(h_kr[:, None, :, :], (B, MLA_HEADS, S, MLA_ROPE))
        k_a = jnp.concatenate([k_nope, k_rope], axis=-1)
        q_a = _rms_norm(q_a, mla_q_gain[l])
        k_a = _rms_norm(k_a, mla_k_gain[l])
        q_a = jnp.concatenate([q_a[..., :MLA_NOPE], _apply_rope(q_a[..., MLA_NOPE:], cos_mla, sin_mla)], axis=-1)
        k_a = jnp.concatenate([k_a[..., :MLA_NOPE], _apply_rope(k_a[..., MLA_NOPE:], cos_mla, sin_mla)], axis=-1)
        o_a = _dense_attention(q_a, k_a, v_a, MLA_QK ** -0.5,
                               "bhqd,bhsd->bhqs", "bhqs,bhsv->bhqv")
        o_a = o_a.transpose(0, 2, 1, 3).reshape(B, S, MLA_HEADS * MLA_V)

        q_b = h_dq.reshape(B, S, DIFF_HEADS, 2, DIFF_QK).transpose(0, 2, 3, 1, 4)
        k_b = h_dk.reshape(B, S, DIFF_HEADS, 2, DIFF_QK).transpose(0, 2, 3, 1, 4)
        v_b = h_dv.reshape(B, S, DIFF_HEADS, DIFF_V).transpose(0, 2, 1, 3)
        q_b = _apply_rope(_rms_norm(q_b, diff_q_gain[l]), cos_dif, sin_dif)
        k_b = _apply_rope(_rms_norm(k_b, diff_k_gain[l]), cos_dif, sin_dif)
        lambda_init = 0.8 - 0.6 * math.exp(-0.3 * l)
        lam = (jnp.exp(jnp.sum(diff_lq1[l].astype(jnp.float32) * diff_lk1[l].astype(jnp.float32)))
               - jnp.exp(jnp.sum(diff_lq2[l].astype(jnp.float32) * diff_lk2[l].astype(jnp.float32)))
               + lambda_init)
        scale_b = DIFF_QK ** -0.5

        def diff_block(qb, k_b=k_b, v_b=v_b, lam=lam):
            s = jnp.einsum("bhcqd,bhcsd->bhcqs", qb, k_b).astype(jnp.float32) * scale_b
            p = jax.nn.softmax(s, axis=-1)
            a = p[:, :, 0] - lam * p[:, :, 1]
            return jnp.einsum("bhqs,bhsv->bhqv", a.astype(v_b.dtype), v_b)

        o_b = _sweep_query_blocks(diff_block, q_b)
        o_b = (_rms_norm(o_b, diff_out_gain[l]) * (1.0 - lambda_init)).astype(x.dtype)
        o_b = o_b.transpose(0, 2, 1, 3).reshape(B, S, DIFF_HEADS * DIFF_V)

        q_c = h_gq.reshape(B, S, GQA_KV_HEADS, GQA_GROUP, GQA_DIM).transpose(0, 2, 3, 1, 4)
        k_c = h_gk.reshape(B, S, GQA_KV_HEADS, GQA_DIM).transpose(0, 2, 1, 3)
        v_c = h_gv.reshape(B, S, GQA_KV_HEADS, GQA_DIM).transpose(0, 2, 1, 3)
        q_c = axial_rope(_rms_norm(q_c, gqa_q_gain[l]))
        k_c = axial_rope(_rms_norm(k_c, gqa_k_gain[l]))
        o_c = _dense_attention(q_c, k_c, v_c, GQA_DIM ** -0.5,
                               "bkgqd,bksd->bkgqs", "bkgqs,bksv->bkgqv")
        o_c = o_c.transpose(0, 3, 1, 2, 4).reshape(B, S, GQA_HEADS * GQA_DIM)

        mix = jnp.concatenate([o_a, o_b, o_c], axis=-1)
        x = x + mix @ w_o[l]

        xn = _rms_norm(x, ffn_norm[l])
        gate, up = jnp.split(xn @ w_gate_up[l], 2, axis=-1)
        x = x + (jax.nn.silu(gate) * up) @ w_down[l]
    return x
```

```python
import math
import numpy as np
import concourse.bass as bass
import concourse.mybir as mybir
from concourse.bass_utils import run_bass_kernel_spmd

F32 = mybir.dt.float32
BF16 = mybir.dt.bfloat16
AF = mybir.ActivationFunctionType
ALU = mybir.AluOpType

ENGS = ("pe", "act", "dve", "pool", "sp")
STRICT_SAME_ENGINE = True


class View:
    __slots__ = ("ap", "root", "p0", "p1", "c0", "c1")

    def __init__(self, ap, root, p0, p1, c0, c1):
        self.ap, self.root, self.p0, self.p1, self.c0, self.c1 = ap, root, p0, p1, c0, c1


class Buf:
    def __init__(self, ap2d, esz, root, base_bytes=0):
        self.ap, self.esz, self.root, self.base = ap2d, esz, root, base_bytes

    def v(self, c0, c1, p0=0, p1=128):
        return View(self.ap[p0:p1, c0:c1], self.root, p0, p1,
                    self.base + c0 * self.esz, self.base + c1 * self.esz)

    def sub(self, c0, c1):
        return Buf(self.ap[:, c0:c1], self.esz, self.root, self.base + c0 * self.esz)


def _ov(a, v):
    return a[0] < v.p1 and v.p0 < a[1] and a[2] < v.c1 and v.c0 < a[3]


def _inside(a, v):
    return v.p0 <= a[0] and a[1] <= v.p1 and v.c0 <= a[2] and a[3] <= v.c1


class Op:
    __slots__ = ("eng", "fn", "idx", "deps", "dma_key", "signal", "count", "gidx", "wl")


class Prog:
    def __init__(self, psum_roots=("ps",), bank_bytes=2048):
        self.per = {e: [] for e in ENGS}
        self.w = {}
        self.r = {}
        self.nops = 0
        self.psum_roots = set(psum_roots)
        self.bank_bytes = bank_bytes

    def _norm(self, v):
        if v.root not in self.psum_roots:
            return v
        bb = self.bank_bytes
        return View(v.ap, v.root, 0, 128, (v.c0 // bb) * bb, -(-v.c1 // bb) * bb)

    def add(self, eng, fn, reads=(), writes=(), dma_key=None):
        op = Op()
        op.eng, op.fn, op.dma_key, op.signal, op.count = eng, fn, dma_key, False, 0
        op.idx = len(self.per[eng])
        op.gidx = self.nops
        me = (eng, op.idx)
        deps = {}
        is_dma = dma_key is not None

        def dep(a, kind):
            key = (a[4], a[5])
            if key == me:
                return
            if kind != "raw" and deps.get(key) == "raw":
                return
            deps[key] = kind if deps.get(key) != "raw" else "raw"

        reads = [self._norm(v) for v in reads]
        writes = [self._norm(v) for v in writes]
        for v in reads:
            for a in self.w.get(v.root, ()):
                if _ov(a, v):
                    dep(a, "raw")
            if v.root in self.psum_roots:
                for a in self.r.get(v.root, ()):
                    if a[4] != eng and _ov(a, v):
                        dep(a, "rar")
        for v in writes:
            for a in self.w.get(v.root, ()):
                if _ov(a, v):
                    dep(a, "waw")
            for a in self.r.get(v.root, ()):
                if _ov(a, v):
                    dep(a, "war")
        for v in reads:
            rl = self.r.setdefault(v.root, [])
            if not is_dma:
                rl[:] = [a for a in rl if not (a[4] == eng and not a[6] and _inside(a, v))]
            rl.append((v.p0, v.p1, v.c0, v.c1, eng, op.idx, is_dma))
        for v in writes:
            wl = self.w.setdefault(v.root, [])
            rl = self.r.setdefault(v.root, [])
            wl[:] = [a for a in wl if not _inside(a, v)]
            rl[:] = [a for a in rl if not _inside(a, v)]
            wl.append((v.p0, v.p1, v.c0, v.c1, eng, op.idx, is_dma))
        op.deps = deps
        self.per[eng].append(op)
        self.nops += 1
        return op

    def emit(self, nc, block_ctx, sems, dma_sems):
        per = self.per
        allops = sorted((op for e in ENGS for op in per[e]), key=lambda o: o.gidx)
        clock_done = {}
        last_start = {e: {} for e in ENGS}
        last_comp = {e: None for e in ENGS}
        for op in allops:
            e = op.eng
            known = dict(last_start[e])
            cands = {}
            dma_c = []
            for (te, ti), kind in op.deps.items():
                top = per[te][ti]
                if top.dma_key is not None:
                    dma_c.append(top)
                    continue
                if te == e:
                    if e == "pe" or (kind != "raw" and not STRICT_SAME_ENGINE):
                        continue
                if te not in cands or cands[te].idx < ti:
                    cands[te] = top
            cl = [t for t in cands.values() if known.get(t.eng, -1) < t.idx]
            keep = []
            for c in cl:
                covered = False
                for c2 in cl:
                    if c2 is not c and clock_done[(c2.eng, c2.idx)].get(c.eng, -1) >= c.idx:
                        covered = True
                        break
                if not covered:
                    keep.append(c)
            for c in keep:
                c.signal = True
                for t_, i_ in clock_done[(c.eng, c.idx)].items():
                    if known.get(t_, -1) < i_:
                        known[t_] = i_
                if known.get(c.eng, -1) < c.idx:
                    known[c.eng] = c.idx
            op.wl = keep + dma_c
            last_start[e] = known
            done = dict(known)
            if op.dma_key is None:
                prev = last_comp[e]
                if prev is not None:
                    for t_, i_ in clock_done[(e, prev.idx)].items():
                        if done.get(t_, -1) < i_:
                            done[t_] = i_
                    if done.get(e, -1) < prev.idx:
                        done[e] = prev.idx
                last_comp[e] = op
            clock_done[(e, op.idx)] = done
        waits = {e: [op.wl for op in per[e]] for e in ENGS}
        for e in ENGS:
            c = 0
            for op in per[e]:
                if op.dma_key is None and op.signal:
                    c += 1
                    op.count = c
        dcount = {}
        for e in ENGS:
            for op in per[e]:
                if op.dma_key is not None:
                    dcount[op.dma_key] = dcount.get(op.dma_key, 0) + 16
                    op.count = dcount[op.dma_key]
        self.stats = {e: (len(per[e]), sum(1 for o in per[e] if o.signal),
                          sum(len(w) for w in waits[e])) for e in ENGS}

        def run(e, handle):
            dseen = {}
            for op, wl in zip(per[e], waits[e]):
                for top in wl:
                    if top.dma_key is not None:
                        if dseen.get(top.dma_key, 0) >= top.count:
                            continue
                        dseen[top.dma_key] = top.count
                        handle.wait_ge(dma_sems[top.dma_key], top.count)
                    else:
                        handle.wait_ge(sems[top.eng], top.count)
                ins = op.fn(handle)
                if op.dma_key is not None:
                    ins.then_inc(dma_sems[op.dma_key], 16)
                elif op.signal:
                    ins.then_inc(sems[e], 1)

        @block_ctx.tensor
        def _(h):
            run("pe", h)

        @block_ctx.scalar
        def _(h):
            run("act", h)

        @block_ctx.vector
        def _(h):
            run("dve", h)

        @block_ctx.gpsimd
        def _(h):
            run("pool", h)

        @block_ctx.sync
        def _(h):
            run("sp", h)
            for k, c in dcount.items():
                h.wait_ge(dma_sems[k], c)


class K:
    def __init__(self, prog):
        self.p = prog

    def mm(self, out, lhsT, rhs, start=True, stop=True):
        self.p.add("pe", lambda e: e.matmul(out.ap, lhsT.ap, rhs.ap, start=start, stop=stop, skip_group_check=True),
                   reads=(lhsT, rhs), writes=(out,))

    def act(self, out, in_, func, scale=1.0, bias=0.0):
        reads = [in_]
        sc = scale
        bi = bias
        if isinstance(scale, View):
            reads.append(scale)
            sc = scale.ap
        if isinstance(bias, View):
            reads.append(bias)
            bi = bias.ap
        self.p.add("act", lambda e: e.activation(out.ap, in_.ap, func, bias=bi, scale=sc),
                   reads=reads, writes=(out,))

    def tt(self, out, in0, in1, op, eng="dve"):
        self.p.add(eng, lambda e: e.tensor_tensor(out.ap, in0.ap, in1.ap, op),
                   reads=(in0, in1), writes=(out,))

    def ts(self, out, in0, s1, s2, op0, op1=None, eng="dve"):
        reads = [in0]
        a1, a2 = s1, s2
        if isinstance(s1, View):
            reads.append(s1)
            a1 = s1.ap
        if isinstance(s2, View):
            reads.append(s2)
            a2 = s2.ap
        if op1 is None:
            self.p.add(eng, lambda e: e.tensor_scalar(out.ap, in0.ap, a1, None, op0),
                       reads=reads, writes=(out,))
        else:
            self.p.add(eng, lambda e: e.tensor_scalar(out.ap, in0.ap, a1, a2, op0, op1),
                       reads=reads, writes=(out,))

    def stt(self, out, in0, scalar, in1, op0, op1, eng="dve"):
        reads = [in0, in1]
        s = scalar
        if isinstance(scalar, View):
            reads.append(scalar)
            s = scalar.ap
        self.p.add(eng, lambda e: e.scalar_tensor_tensor(out.ap, in0.ap, s, in1.ap, op0, op1),
                   reads=reads, writes=(out,))

    def copy(self, out, in_, eng="dve"):
        self.p.add(eng, lambda e: e.tensor_copy(out.ap, in_.ap), reads=(in_,), writes=(out,))

    def recip(self, out, in_):
        self.p.add("dve", lambda e: e.reciprocal(out.ap, in_.ap), reads=(in_,), writes=(out,))

    def memset(self, out, val, eng="dve"):
        self.p.add(eng, lambda e: e.memset(out.ap, val), reads=(), writes=(out,))

    def reduce_sum(self, out, in_):
        self.p.add("dve", lambda e: e.reduce_sum(out.ap, in_.ap, mybir.AxisListType.X),
                   reads=(in_,), writes=(out,))

    def dma_in(self, queue, key, out, dram_ap):
        self.p.add(queue, lambda e: e.dma_start(out=out.ap, in_=dram_ap), reads=(), writes=(out,),
                   dma_key=key)

    def dma_out(self, queue, key, dram_ap, in_):
        self.p.add(queue, lambda e: e.dma_start(out=dram_ap, in_=in_.ap), reads=(in_,), writes=(),
                   dma_key=key)


import collections
from contextlib import ExitStack

S = 2048
D = 1024
TT = 512
NT = 4
KC = 8
DEPTH = 2
FF = 2816
NJ = 22
EPS = 1e-6
THETA = 10000.0
NSLOT = 3
SLOT = 3072
NV = 284
NCST = 7
A_MIX = 0
A_EXP = 8192
A_QT = A_EXP + 3 * 1024
A_KV = A_QT + 4 * 512
ARENA = A_KV + 14336
MLAW_N = 3 * 768 + 2 * 768 + 2 * 384
POOL_MLA = True


def layer_blocks():
    b = []
    for h in range(4):
        b.append(("dw%d" % h, 8 * 384))
    b += [("gwkv", 8 * 384), ("gwq", 8 * 384), ("woda", 4 * 512), ("wodb", 4 * 512), ("wog", 3 * 1024),
          ("mcq", 8 * 384), ("mckv", 8 * 256), ("mkr", 8 * 128), ("mlaw", MLAW_N),
          ("wom", 3 * 1024)]
    for j in range(NJ):
        b.append(("wgu%d" % j, 8 * 256))
    for dc in range(8):
        b.append(("wd%d" % dc, NJ * 128))
    off = {}
    o = 0
    for n, c in b:
        off[n] = (o, c)
        o += c
    return b, off, o


BLOCKS, BOFF, TOTW = layer_blocks()


def lambda_init(l):
    return 0.8 - 0.6 * math.exp(-0.3 * l)


class WStream:
    def __init__(self, k, wsl, w_d, seq):
        self.k, self.wsl, self.w_d, self.seq = k, wsl, w_d, seq
        self.issued = 0
        self.live = {}
        self.free = list(range(NSLOT))
        for _ in range(NSLOT):
            self._issue()

    def _issue(self):
        if self.issued >= len(self.seq) or not self.free:
            return
        l, name = self.seq[self.issued]
        slot = self.free.pop(0)
        off, n = BOFF[name]
        buf = self.wsl.sub(slot * SLOT, slot * SLOT + n)
        self.k.dma_in("pool", "ws%d" % slot, buf.v(0, n), self.w_d[l][:, off:off + n])
        self.live[self.issued] = (slot, buf, l, name)
        self.issued += 1

    def get(self, l, name):
        for i, (slot, buf, ll, nn) in self.live.items():
            if ll == l and nn == name:
                return buf
        raise RuntimeError("weight block %s/%d not resident; live=%s" % (name, l, [(v[2], v[3]) for v in self.live.values()]))

    def done(self, l, name):
        for i, (slot, buf, ll, nn) in list(self.live.items()):
            if ll == l and nn == name:
                del self.live[i]
                self.free.append(slot)
                self._issue()
                return
        raise RuntimeError("done: %s not live" % name)


class _Stop(Exception):
    pass


def build_program(n_layers=DEPTH, first_layer=0, stop_after=None):
    nc = bass.Bass("TRN2", target_bir_lowering=False)
    xT_d = nc.dram_tensor("xT", [128, KC * S], F32, kind="ExternalInput").ap()
    yT_d = nc.dram_tensor("yT", [128, KC * S], F32, kind="ExternalOutput").ap()
    w_d = [nc.dram_tensor("w%d" % l, [128, TOTW], F32, kind="ExternalInput").ap() for l in range(n_layers)]
    v_d = [nc.dram_tensor("v%d" % l, [128, NV], F32, kind="ExternalInput").ap() for l in range(n_layers)]
    tab_d = nc.dram_tensor("tab", [128, 6 * S], BF16, kind="ExternalInput").ap()
    cst_d = nc.dram_tensor("cst", [128, NCST * 128], BF16, kind="ExternalInput").ap()

    with ExitStack() as es:
        def sb(name, cols, dt):
            return es.enter_context(nc.sbuf_tensor(name, [128, cols], dt))
        xT_t = sb("xT_s", KC * S, F32)
        xn_t = sb("xn_s", KC * S, BF16)
        tab_t = sb("tab_s", 2 * 2 * S, BF16)
        wsl_t = sb("wsl_s", NSLOT * SLOT, BF16)
        ar_t = sb("arena_s", ARENA, BF16)
        cst_t = sb("cst_s", NCST * 128, BF16)
        vec_t = sb("vec_s", n_layers * NV, F32)
        scb_t = sb("scb_s", 4 * 512, BF16)
        scf_t = sb("scf_s", 7 * 512, F32)
        sm_t = sb("sm_s", 16, F32)
        ps_t = es.enter_context(nc.psum_tensor("ps", [128, 8 * 512], F32))
        dma_keys = ["ws%d" % i for i in range(NSLOT)] + ["xin%d" % i for i in range(KC)] + \
                   ["yout", "tab0", "tab1", "cst", "mlaw"] + ["vec%d" % i for i in range(n_layers)]
        sems = {e: es.enter_context(nc.semaphore("sem_" + e)) for e in ENGS}
        dsems = {kk: es.enter_context(nc.semaphore("dsem_" + kk)) for kk in dma_keys}
        block = es.enter_context(nc.Block())

        P = Prog()
        k = K(P)
        xT = Buf(xT_t[:], 4, "xT")
        xn = Buf(xn_t[:], 2, "xn")
        tab = Buf(tab_t[:], 2, "tab")
        wsl = Buf(wsl_t[:], 2, "wsl")
        ar = Buf(ar_t[:], 2, "arena")
        cst = Buf(cst_t[:], 2, "cst")
        vec = Buf(vec_t[:], 4, "vec")
        scb = Buf(scb_t[:], 2, "scb")
        scf = Buf(scf_t[:], 4, "scf")
        sm = Buf(sm_t[:], 4, "sm")
        ps = Buf(ps_t[:], 4, "ps")

        def bank(i, p0=0, p1=128, c0=0, c1=512):
            return ps.v(i * 512 + c0, i * 512 + c1, p0, p1)

        ONES = cst.sub(0, 128)
        BD64 = cst.sub(128, 256)
        R_DIFF = cst.sub(256, 384)
        R_GQA = cst.sub(384, 512)
        R_MLA = cst.sub(512, 640)
        SELKR = cst.sub(640, 768)
        ONES96 = cst.sub(768, 896)

        class SS:
            def __init__(self, i):
                self.bfA = scb.sub((2 * i) * 512, (2 * i + 1) * 512)
                self.bfB = scb.sub((2 * i + 1) * 512, (2 * i + 2) * 512)
                self.f0 = scf.sub((3 * i) * 512, (3 * i + 1) * 512)
                self.f1 = scf.sub((3 * i + 1) * 512, (3 * i + 2) * 512)
                self.f2 = scf.sub((3 * i + 2) * 512, (3 * i + 3) * 512)
        SSETS = [SS(0), SS(1)]
        EF0 = scf.sub(6 * 512, 7 * 512)
        ss_ctr = [0]

        def next_ss():
            ss_ctr[0] += 1
            return SSETS[ss_ctr[0] % 2]

        bgq = collections.deque()

        def place(stages, start=0, stride=1):
            for s_, fn in enumerate(stages):
                idx = start + s_ * stride
                while len(bgq) <= idx:
                    bgq.append([])
                bgq[idx].append(fn)

        def tick(n=1):
            for _ in range(n):
                if bgq:
                    for fn in bgq.popleft():
                        fn()

        def drain():
            while bgq:
                tick()

        def run_interleaved(tasks):
            tasks = [list(t) for t in tasks]
            m = max(len(t) for t in tasks)
            for s_ in range(m):
                for t in tasks:
                    if s_ < len(t):
                        t[s_]()

        k.dma_in("sp", "cst", cst.v(0, NCST * 128), cst_d)
        for l in range(n_layers):
            k.dma_in("sp", "vec%d" % l, vec.v(l * NV, (l + 1) * NV), v_d[l])
        for c in range(KC):
            k.dma_in("sp", "xin%d" % c, xT.v(c * S, (c + 1) * S), xT_d[:, c * S:(c + 1) * S])
        seq = []
        for l in range(n_layers):
            for n, _ in BLOCKS:
                if n == "mlaw" or n.startswith("wgu") or n.startswith("wd"):
                    continue
                seq.append((l, n))
            for half in range(2):
                for j in range(NJ):
                    seq.append((l, "wgu%d" % j))
                for dc in range(8):
                    seq.append((l, "wd%d" % dc))
        ws = WStream(k, wsl, w_d, seq)

        tab_phase = [0]

        def load_tab(kind):
            i = tab_phase[0] % 2
            tab_phase[0] += 1
            base = i * 2 * S
            k.dma_in("sp", "tab%d" % i, tab.v(base, base + 2 * S), tab_d[:, kind * 2 * S:(kind + 1) * 2 * S])
            return tab.sub(base, base + S), tab.sub(base + S, base + 2 * S)

        def vcol(l, c, p0=0, p1=128):
            return vec.v(l * NV + c, l * NV + c + 1, p0, p1)

        def rstd_from(dst, ms_view, inv_d):
            k.act(dst, ms_view, AF.Ln, scale=inv_d, bias=EPS)
            k.act(dst, dst, AF.Exp, scale=-0.5)

        def rmsnorm_x(l, gbase):
            for t in range(NT):
                Sx = next_ss()
                b = t % 2
                sq = [Sx.bfA, Sx.bfB]
                for c in range(KC):
                    k.act(sq[c % 2].v(0, TT), xT.v(c * S + t * TT, c * S + (t + 1) * TT), AF.Square)
                    k.mm(bank(b), ONES.v(0, 128), sq[c % 2].v(0, TT), start=(c == 0), stop=(c == KC - 1))
                rstd_from(Sx.f1.v(0, TT), bank(b), 1.0 / D)
                for c in range(KC):
                    k.stt(xn.v(c * S + t * TT, c * S + (t + 1) * TT), xT.v(c * S + t * TT, c * S + (t + 1) * TT),
                          vcol(l, gbase + c), Sx.f1.v(0, TT), ALU.mult, ALU.mult)

        def chain_stages(X, Y, proj_fn, gcol, cosb, sinb, t, Rm, OM, inv_d, dsts, after=None, pool=False):
            st = {}

            def A():
                proj_fn(X)

            def B():
                Sx = st["S"] = next_ss()
                k.act(Sx.bfA.v(0, TT), bank(X), AF.Square)
                k.ts(Sx.bfB.v(0, TT), bank(X), gcol, None, ALU.mult)

            def C():
                Sx = st["S"]
                k.mm(bank(Y), OM.v(0, 128), Sx.bfA.v(0, TT))
                k.mm(bank(X), Rm.v(0, 128), Sx.bfB.v(0, TT))

            def Dd():
                Sx = st["S"]
                rstd_from(Sx.f1.v(0, TT), bank(Y), inv_d)
                pe_ = "pool" if pool else "dve"
                k.tt(Sx.f0.v(0, TT), Sx.bfB.v(0, TT), cosb.v(t * TT, (t + 1) * TT), ALU.mult, eng=pe_)
                k.tt(Sx.f2.v(0, TT), bank(X), sinb.v(t * TT, (t + 1) * TT), ALU.mult)
                k.tt(Sx.f0.v(0, TT), Sx.f0.v(0, TT), Sx.f2.v(0, TT), ALU.add, eng=pe_)
                for (db, c0, p0, p1) in dsts:
                    k.tt(db.v(c0, c0 + TT, p0, p1), Sx.f0.v(0, TT, p0, p1), Sx.f1.v(0, TT, p0, p1), ALU.mult)
                if after is not None:
                    after()
            return [A, B, C, Dd]

        def proj(dst_bank, wblk, wstride, wc0, M, t):
            for kc in range(KC):
                k.mm(bank(dst_bank, 0, M), wblk.v(kc * wstride + wc0, kc * wstride + wc0 + M),
                     xn.v(kc * S + t * TT, kc * S + (t + 1) * TT), start=(kc == 0), stop=(kc == KC - 1))

        EXPT = [ar.sub(A_EXP + i * 1024, A_EXP + (i + 1) * 1024) for i in range(3)]
        exp_ctr = [0]
        QT = [ar.sub(A_QT + i * 512, A_QT + (i + 1) * 512) for i in range(4)]
        MIX = [ar.sub(A_MIX + i * S, A_MIX + (i + 1) * S) for i in range(4)]
        KVR = ar.sub(A_KV, A_KV + 14336)

        def attn_pairs(npairs, score_fn, pv_fn, scale, nt=1):
            def S_(i):
                pb = 2 * (i % 2)
                score_fn(i, pb, pb + 1)
            S_(0)
            if npairs > 1:
                S_(1)
            for i in range(npairs):
                pb = 2 * (i % 2)
                eb = EXPT[exp_ctr[0] % 3]
                exp_ctr[0] += 1
                k.act(eb.v(0, 1024), ps.v(pb * 512, (pb + 2) * 512), AF.Exp, scale=scale)
                tick(nt)
                if i + 2 < npairs:
                    S_(i + 2)
                pv_fn(i, eb)

        def wo_stage(l, nch, blocks, per_block_dc):
            cnt = 0
            for bi, bn in enumerate(blocks):
                blk = ws.get(l, bn)
                width = per_block_dc * 128
                for dci in range(per_block_dc):
                    dc = bi * per_block_dc + dci
                    for t in range(NT):
                        b = cnt % 4
                        cnt += 1
                        for c in range(nch):
                            k.mm(bank(b), blk.v(c * width + dci * 128, c * width + (dci + 1) * 128),
                                 MIX[c].v(t * TT, (t + 1) * TT), start=(c == 0), stop=(c == nch - 1))
                        xv = xT.v(dc * S + t * TT, dc * S + (t + 1) * TT)
                        k.tt(xv, xv, bank(b), ALU.add)
                ws.done(l, bn)

        def phase(name):
            if stop_after == name:
                raise _Stop()

        def layer_body(l):
            lam0 = lambda_init(l + first_layer)
            Sx = next_ss()
            for i, (ca, cb) in enumerate(((28, 92), (156, 220))):
                k.tt(Sx.f0.v(0, 64), vec.v(l * NV + ca, l * NV + ca + 64), vec.v(l * NV + cb, l * NV + cb + 64), ALU.mult)
                k.reduce_sum(sm.v(i, i + 1), Sx.f0.v(0, 64))
                k.act(sm.v(2 + i, 3 + i), sm.v(i, i + 1), AF.Exp)
            k.tt(sm.v(4, 5), sm.v(2, 3), sm.v(3, 4), ALU.subtract)
            k.ts(sm.v(5, 6), sm.v(4, 5), -1.0, -lam0, ALU.mult, ALU.add)
            k.ts(sm.v(6, 7), vcol(l, 25), 1.0 - lam0, None, ALU.mult)
            NLAM = sm.v(5, 6)
            GSC = sm.v(6, 7)

            rmsnorm_x(l, 0)
            k.memset(ar.v(A_QT, A_QT + 4 * 512), 0.0, eng="pool")
            phase('norm')

            dcos, dsin = load_tab(0)
            gcos, gsin = load_tab(1)
            KD = [KVR.sub(0, 2048), KVR.sub(2048, 4096)]
            VG = [KVR.sub(8192, 8192 + 3072), KVR.sub(8192 + 3072, 8192 + 6144)]
            k.memset(VG[0].v(0, 3072), 1.0, eng="pool")
            k.memset(VG[1].v(0, 3072), 1.0, eng="pool")

            def dkv(h):
                par = h % 2
                return KVR.sub(par * 4096, par * 4096 + 2048), KVR.sub(par * 4096 + 2048, par * 4096 + 4096)

            def diff_k_stages(h, t, X, Y):
                kT, vv = dkv(h)
                return chain_stages(X, Y, lambda b_: proj(b_, ws.get(l, "dw%d" % h), 384, 128, 128, t),
                                    vcol(l, 24), dcos, dsin, t, R_DIFF, BD64, 1.0 / 64, [(kT, t * TT, 0, 128)])

            def diff_v_stages(h, t, Z):
                kT, vv = dkv(h)

                def A():
                    blk = ws.get(l, "dw%d" % h)
                    for sc in range(4):
                        tok0 = t * TT + sc * 128
                        for kc in range(KC):
                            k.mm(bank(Z, 0, 128, sc * 128, (sc + 1) * 128), xn.v(kc * S + tok0, kc * S + tok0 + 128),
                                 blk.v(kc * 384 + 256, kc * 384 + 384), start=(kc == 0), stop=(kc == KC - 1))

                def B():
                    k.copy(vv.v(t * TT, (t + 1) * TT), bank(Z))
                return [A, B]

            def diff_q_stages(h, t, qi, X, Y, after=None):
                return chain_stages(X, Y, lambda b_: proj(b_, ws.get(l, "dw%d" % h), 384, 0, 128, t),
                                    vcol(l, 23), dcos, dsin, t, R_DIFF, BD64, 1.0 / 64,
                                    [(QT[2 * qi], 0, 0, 64), (QT[2 * qi + 1], 0, 64, 128)], after=after)

            def gqa_k_stages(t, g, X, Y):
                return chain_stages(X, Y, lambda b_: proj(b_, ws.get(l, "gwkv"), 384, g * 128, 128, t),
                                    vcol(l, 27), gcos, gsin, t, R_GQA, BD64, 1.0 / 64, [(KD[g], t * TT, 0, 128)])

            def gqa_v_stages(t, Z, last=False):
                def A():
                    blk = ws.get(l, "gwkv")
                    for sc in range(4):
                        tok0 = t * TT + sc * 128
                        for kc in range(KC):
                            k.mm(bank(Z, 0, 128, sc * 128, (sc + 1) * 128), xn.v(kc * S + tok0, kc * S + tok0 + 128),
                                 blk.v(kc * 384 + 256, kc * 384 + 384), start=(kc == 0), stop=(kc == KC - 1))
                    if last:
                        ws.done(l, "gwkv")

                def B():
                    for sc in range(4):
                        for g in range(2):
                            c0 = (t * 4 + sc) * 192 + 64
                            k.copy(VG[g].v(c0, c0 + 64), bank(Z, 0, 128, sc * 128 + g * 64, sc * 128 + (g + 1) * 64))
                return [A, B]

            def gqa_q_stages(c, t, qi, X, Y, after=None):
                return chain_stages(X, Y, lambda b_: proj(b_, ws.get(l, "gwq"), 384, c * 128, 128, t),
                                    vcol(l, 26), gcos, gsin, t, R_GQA, BD64, 1.0 / 64,
                                    [(QT[2 * qi], 0, 0, 64), (QT[2 * qi + 1], 0, 64, 128)], after=after)

            EXP5 = [ar.sub(A_EXP + i * 512, A_EXP + (i + 1) * 512) for i in range(6)]

            def diff_unit(h, t8, ui):
                kT, vv = dkv(h)
                t, half = t8 // 2, t8 % 2
                pq = (h * 4 + t) % 2
                qA, qB = QT[2 * pq], QT[2 * pq + 1]
                c0 = half * 256
                OB, SM = (2, 3) if ui % 2 == 0 else (4, 5)

                def S_(i):
                    b = i % 2
                    k.mm(bank(b, 0, 128, 0, 256), kT.v(i * 128, (i + 1) * 128), qA.v(c0, c0 + 256), start=True, stop=True)
                    k.mm(bank(b, 0, 128, 256, 512), kT.v(i * 128, (i + 1) * 128), qB.v(c0, c0 + 256), start=False, stop=True)
                S_(0)
                S_(1)
                for i in range(16):
                    eb = EXP5[exp_ctr[0] % 6]
                    exp_ctr[0] += 1
                    k.act(eb.v(0, 512), bank(i % 2), AF.Exp, scale=0.125)
                    first, last = (i == 0), (i == 15)
                    tick()
                    if i + 2 < 16:
                        S_(i + 2)
                    k.mm(bank(OB, 0, 128, 0, 256), vv.v(i * 128, (i + 1) * 128), eb.v(0, 256), start=first, stop=last)
                    k.mm(bank(OB, 0, 128, 256, 512), vv.v(i * 128, (i + 1) * 128), eb.v(256, 512), start=False, stop=last)
                    k.mm(bank(SM, 0, 128, 0, 256), ONES.v(0, 128), eb.v(0, 256), start=first, stop=last)
                    k.mm(bank(SM, 0, 128, 256, 512), ONES.v(0, 128), eb.v(256, 512), start=False, stop=last)
                Sx = next_ss()
                k.recip(EF0.v(0, 512), bank(SM))
                k.tt(Sx.f0.v(0, 256), bank(OB, 0, 128, 0, 256), EF0.v(0, 256), ALU.mult)
                k.tt(Sx.f2.v(0, 256), bank(OB, 0, 128, 256, 512), EF0.v(256, 512), ALU.mult)
                k.stt(Sx.f0.v(0, 256), Sx.f2.v(0, 256), NLAM, Sx.f0.v(0, 256), ALU.mult, ALU.add)

                def Esq():
                    k.tt(Sx.bfA.v(0, 256), Sx.f0.v(0, 256), Sx.f0.v(0, 256), ALU.mult)

                def E2():
                    k.mm(bank(SM, 0, 128, 0, 256), ONES.v(0, 128), Sx.bfA.v(0, 256))

                def E3():
                    rstd_from(Sx.f1.v(0, 256), bank(SM, 0, 128, 0, 256), 1.0 / 128)
                    k.stt(MIX[h].v(t8 * 256, (t8 + 1) * 256), Sx.f0.v(0, 256), GSC, Sx.f1.v(0, 256), ALU.mult, ALU.mult)
                return [Esq, E2, E3]

            run_interleaved([diff_k_stages(0, 0, 0, 1), diff_k_stages(0, 1, 2, 3)])
            run_interleaved([diff_k_stages(0, 2, 0, 1), diff_k_stages(0, 3, 2, 3), diff_v_stages(0, 0, 4), diff_v_stages(0, 1, 5)])
            run_interleaved([diff_q_stages(0, 0, 0, 6, 7), diff_v_stages(0, 2, 4), diff_v_stages(0, 3, 5)])
            phase('diffpro')
            tiles = [(h, t) for h in range(4) for t in range(NT)]
            ui = 0
            epi = None
            for ti, (h, t) in enumerate(tiles):
                pq_next = (ti + 1) % 2
                for half in range(2):
                    if epi is not None:
                        place(epi, 8, 2)
                    if half == 0:
                        if ti + 1 < len(tiles):
                            nh, nt_ = tiles[ti + 1]
                            place(diff_q_stages(nh, nt_, pq_next, 6, 7), 0, 2)
                        else:
                            place(gqa_q_stages(0, 0, pq_next, 6, 7), 0, 2)
                        if h == 3:
                            place(gqa_k_stages(t, 1, 6, 7), 8, 2)
                    else:
                        if h + 1 < 4:
                            place(diff_k_stages(h + 1, t, 6, 7), 0, 2)
                            place(diff_v_stages(h + 1, t, 7), 8, 2)
                        else:
                            place(gqa_k_stages(t, 0, 6, 7), 0, 2)
                            place(gqa_v_stages(t, 7, last=(t == NT - 1)), 8, 2)
                    epi = diff_unit(h, t * 2 + half, ui)
                    ui += 1
                    if ui == 1:
                        phase('diffu0')
                if t == NT - 1 and h < 3:
                    ws.done(l, "dw%d" % h)
            place(epi, 0, 2)
            drain()
            ws.done(l, "dw3")
            wo_stage(l, 4, ["woda", "wodb"], 4)
            phase('diff')

            mcos, msin = load_tab(2)

            def gqa_unit(c, t, qi, aA, aB):
                ge, go = (2 * c) // 3, (2 * c + 1) // 3
                qA, qB = QT[2 * qi], QT[2 * qi + 1]

                def sf(i, bA, bB):
                    k.mm(bank(bA), KD[ge].v(i * 128, (i + 1) * 128), qA.v(0, TT))
                    k.mm(bank(bB), KD[go].v(i * 128, (i + 1) * 128), qB.v(0, TT))

                def pv(i, eb):
                    st, sp_ = (i == 0), (i == 15)
                    k.mm(bank(aA), VG[ge].v(i * 192 + 64, i * 192 + 192), eb.v(0, 512), start=st, stop=sp_)
                    k.mm(bank(aB), VG[go].v(i * 192, i * 192 + 128), eb.v(512, 1024), start=st, stop=sp_)
                attn_pairs(16, sf, pv, 0.125)
                k.recip(EF0.v(0, TT, 0, 64), bank(aA, 64, 128))
                k.tt(MIX[c].v(t * TT, (t + 1) * TT, 0, 64), bank(aA, 0, 64), EF0.v(0, TT, 0, 64), ALU.mult)
                k.recip(EF0.v(0, TT, 64, 128), bank(aB, 0, 64))
                k.tt(MIX[c].v(t * TT, (t + 1) * TT, 64, 128), bank(aB, 64, 128), EF0.v(0, TT, 64, 128), ALU.mult)

            units = [(c, t) for c in range(3) for t in range(NT)]
            for ui, (c, t) in enumerate(units):
                qi = (16 + ui) % 2
                aA, aB = (6, 7) if ui % 2 == 0 else (4, 5)
                oA, oB = (4, 5) if ui % 2 == 0 else (6, 7)
                if ui + 1 < len(units):
                    place(gqa_q_stages(units[ui + 1][0], units[ui + 1][1], (16 + ui + 1) % 2, oA, oB), 6, 2)
                gqa_unit(c, t, qi, aA, aB)
            drain()
            ws.done(l, "gwq")
            wo_stage(l, 3, ["wog"], 8)
            phase('gqa')

            CQN = [xn.sub(m * S, (m + 1) * S) for m in range(3)]
            CKVN = [xn.sub((3 + m) * S, (4 + m) * S) for m in range(2)]
            MLAW = xn.sub(5 * S, 5 * S + MLAW_N)
            KRT = MIX[3]
            bq, bkv, bkr = ws.get(l, "mcq"), ws.get(l, "mckv"), ws.get(l, "mkr")
            for t in range(NT):
                for m in range(3):
                    proj(m, bq, 384, m * 128, 128, t)
                for m in range(2):
                    proj(4 + m, bkv, 256, m * 128, 128, t)
                proj(6, bkr, 128, 0, 128, t)
                Sx = next_ss()
                for m in range(3):
                    k.act(Sx.bfA.v(0, TT), bank(m), AF.Square)
                    k.mm(bank(3), ONES.v(0, 128), Sx.bfA.v(0, TT), start=(m == 0), stop=(m == 2))
                for m in range(2):
                    k.act(Sx.bfB.v(0, TT), bank(4 + m), AF.Square)
                    k.mm(bank(7), ONES.v(0, 128), Sx.bfB.v(0, TT), start=(m == 0), stop=(m == 1))
                rstd_from(Sx.f0.v(0, TT), bank(3), 1.0 / 384)
                rstd_from(Sx.f1.v(0, TT), bank(7), 1.0 / 256)
                for m in range(3):
                    k.stt(CQN[m].v(t * TT, (t + 1) * TT), bank(m), vcol(l, 16 + m), Sx.f0.v(0, TT), ALU.mult, ALU.mult)
                for m in range(2):
                    k.stt(CKVN[m].v(t * TT, (t + 1) * TT), bank(4 + m), vcol(l, 19 + m), Sx.f1.v(0, TT), ALU.mult, ALU.mult)
                k.copy(KRT.v(t * TT, (t + 1) * TT), bank(6))
            ws.done(l, "mcq")
            ws.done(l, "mckv")
            ws.done(l, "mkr")
            moff, mn = BOFF["mlaw"]
            k.dma_in("pool", "mlaw", MLAW.v(0, MLAW_N), w_d[l][:, moff:moff + mn])
            WUQ = MLAW.sub(0, 2304)
            WUKK = MLAW.sub(2304, 2304 + 1536)
            WUKV = MLAW.sub(3840, 3840 + 768)

            def mkv(h):
                base = 4096 if h % 2 == 0 else 0
                return KVR.sub(base, base + 2048), KVR.sub(base + 2048, base + 4096)

            def mla_k_stages(h, t, X, Y):
                kT, va = mkv(h)

                def pj(b_):
                    if t == 0:
                        k.memset(va.v(0, 2048), 1.0, eng="pool")
                    for m in range(2):
                        k.mm(bank(b_), WUKK.v(m * 768 + h * 128, m * 768 + (h + 1) * 128), CKVN[m].v(t * TT, (t + 1) * TT),
                             start=(m == 0), stop=False)
                    k.mm(bank(b_), SELKR.v(0, 128), KRT.v(t * TT, (t + 1) * TT), start=False, stop=True)
                return chain_stages(X, Y, pj, vcol(l, 22), mcos, msin, t, R_MLA, ONES96, 1.0 / 96, [(kT, t * TT, 0, 128)], pool=POOL_MLA)

            def mla_v_stages(h, t, Z):
                kT, va = mkv(h)
                voff = 0 if h % 2 == 0 else 64

                def A():
                    for sc in range(4):
                        tok0 = t * TT + sc * 128
                        for m in range(2):
                            k.mm(bank(Z, 0, 128, sc * 64, (sc + 1) * 64), CKVN[m].v(tok0, tok0 + 128),
                                 WUKV.v(m * 384 + h * 64, m * 384 + (h + 1) * 64), start=(m == 0), stop=(m == 1))

                def B():
                    for sc in range(4):
                        c0 = (t * 4 + sc) * 128 + voff
                        k.copy(va.v(c0, c0 + 64), bank(Z, 0, 128, sc * 64, (sc + 1) * 64))
                return [A, B]

            def mla_q_stages(h, t, qi, X, Y):
                def pj(b_):
                    for m in range(3):
                        k.mm(bank(b_), WUQ.v(m * 768 + h * 128, m * 768 + (h + 1) * 128), CQN[m].v(t * TT, (t + 1) * TT),
                             start=(m == 0), stop=(m == 2))
                return chain_stages(X, Y, pj, vcol(l, 21), mcos, msin, t, R_MLA, ONES96, 1.0 / 96, [(QT[qi], 0, 0, 128)], pool=POOL_MLA)

            def mla_unit(h, t, qi, acc):
                kT, va = mkv(h)
                q = QT[qi]

                def sf(i, bA, bB):
                    k.mm(bank(bA), kT.v((2 * i) * 128, (2 * i + 1) * 128), q.v(0, TT))
                    k.mm(bank(bB), kT.v((2 * i + 1) * 128, (2 * i + 2) * 128), q.v(0, TT))

                def pv(i, eb):
                    k.mm(bank(acc), va.v((2 * i) * 128, (2 * i + 1) * 128), eb.v(0, 512), start=(i == 0), stop=False)
                    k.mm(bank(acc), va.v((2 * i + 1) * 128, (2 * i + 2) * 128), eb.v(512, 1024), start=False, stop=(i == 7))
                attn_pairs(8, sf, pv, 96 ** -0.5, nt=2)
                c = h // 2
                if h % 2 == 0:
                    k.recip(EF0.v(0, TT, 0, 64), bank(acc, 64, 128))
                    k.tt(MIX[c].v(t * TT, (t + 1) * TT, 0, 64), bank(acc, 0, 64), EF0.v(0, TT, 0, 64), ALU.mult)
                else:
                    k.recip(EF0.v(0, TT, 64, 128), bank(acc, 0, 64))
                    k.tt(MIX[c].v(t * TT, (t + 1) * TT, 64, 128), bank(acc, 64, 128), EF0.v(0, TT, 64, 128), ALU.mult)

            run_interleaved([mla_k_stages(0, 0, 0, 1), mla_k_stages(0, 1, 2, 3)])
            run_interleaved([mla_k_stages(0, 2, 0, 1), mla_k_stages(0, 3, 2, 3), mla_v_stages(0, 0, 4), mla_v_stages(0, 1, 5)])
            run_interleaved([mla_q_stages(0, 0, 0, 6, 7), mla_v_stages(0, 2, 4), mla_v_stages(0, 3, 5)])
            units = [(h, t) for h in range(6) for t in range(NT)]
            for ui, (h, t) in enumerate(units):
                acc = 4 + ui % 2
                idle = 5 - ui % 2
                if ui + 1 < len(units):
                    place(mla_q_stages(units[ui + 1][0], units[ui + 1][1], (ui + 1) % 2, 6, 7), 0, 2)
                if h + 1 < 6:
                    place(mla_k_stages(h + 1, t, idle, 7), 8, 2)
                    place(mla_v_stages(h + 1, t, 6), 11, 2)
                mla_unit(h, t, ui % 2, acc)
            drain()
            wo_stage(l, 3, ["wom"], 8)
            phase('mla')

            rmsnorm_x(l, 8)
            ACTT = ar.sub(0, NJ * 1024)
            for half in range(2):
                for j in range(NJ):
                    blk = ws.get(l, "wgu%d" % j)
                    for t2 in range(2):
                        t = half * 2 + t2
                        bG = (j % 2) * 4 + t2 * 2
                        bU = bG + 1
                        for kc in range(KC):
                            k.mm(bank(bG), blk.v(kc * 256, kc * 256 + 128), xn.v(kc * S + t * TT, kc * S + (t + 1) * TT),
                                 start=(kc == 0), stop=(kc == KC - 1))
                        for kc in range(KC):
                            k.mm(bank(bU), blk.v(kc * 256 + 128, kc * 256 + 256), xn.v(kc * S + t * TT, kc * S + (t + 1) * TT),
                                 start=(kc == 0), stop=(kc == KC - 1))
                        Sx = next_ss()
                        k.act(Sx.bfA.v(0, TT), bank(bG), AF.Silu)
                        k.tt(ACTT.v(j * 1024 + t2 * TT, j * 1024 + (t2 + 1) * TT), Sx.bfA.v(0, TT), bank(bU), ALU.mult)
                    ws.done(l, "wgu%d" % j)
                cnt = 0
                for dc in range(8):
                    blk = ws.get(l, "wd%d" % dc)
                    for t2 in range(2):
                        t = half * 2 + t2
                        b = cnt % 4
                        cnt += 1
                        for j in range(NJ):
                            k.mm(bank(b), blk.v(j * 128, (j + 1) * 128), ACTT.v(j * 1024 + t2 * TT, j * 1024 + (t2 + 1) * TT),
                                 start=(j == 0), stop=(j == NJ - 1))
                        xv = xT.v(dc * S + t * TT, dc * S + (t + 1) * TT)
                        k.tt(xv, xv, bank(b), ALU.add)
                    ws.done(l, "wd%d" % dc)

        for l in range(n_layers):
            try:
                layer_body(l)
            except _Stop:
                break

        for c in range(KC):
            k.dma_out("sp", "yout", yT_d[:, c * S:(c + 1) * S], xT.v(c * S, (c + 1) * S))
        P.emit(nc, block, sems, dsems)
        build_program.stats = P.stats
    return nc


def _kc_layout(W):
    c = W.shape[0] // 128
    n = W.shape[1]
    return np.ascontiguousarray(W.reshape(c, 128, n).transpose(1, 0, 2)).reshape(128, c * n)


def _rope_tables():
    pos = np.arange(S, dtype=np.float32)

    def cs(p, dim):
        inv = (1.0 / (THETA ** (np.arange(0, dim, 2, dtype=np.float32) / np.float32(dim)))).astype(np.float32)
        ang = (p[:, None] * inv[None, :]).astype(np.float32)
        return np.cos(ang).astype(np.float32), np.sin(ang).astype(np.float32)

    tabs = np.zeros((128, 6, S), np.float32)
    c, s_ = cs(pos, 64)
    for r in range(128):
        j = (r % 64) % 32
        tabs[r, 0] = c[:, j]
        tabs[r, 1] = s_[:, j]
    cr, sr = cs((np.arange(S) // 64).astype(np.float32), 32)
    cc, sc = cs((np.arange(S) % 64).astype(np.float32), 32)
    for r in range(128):
        i = r % 64
        j = (i % 32) % 16
        if i < 32:
            tabs[r, 2] = cr[:, j]
            tabs[r, 3] = sr[:, j]
        else:
            tabs[r, 2] = cc[:, j]
            tabs[r, 3] = sc[:, j]
    cm, sm_ = cs(pos, 32)
    tabs[:, 4] = 1.0
    tabs[:, 5] = 0.0
    for i in range(32):
        tabs[64 + i, 4] = cm[:, i % 16]
        tabs[64 + i, 5] = sm_[:, i % 16]
    return tabs.reshape(128, 6 * S)


def _rot_matrix(blocks):
    R = np.zeros((128, 128), np.float32)
    for base, half in blocks:
        for i in range(half):
            R[base + half + i, base + i] = -1.0
            R[base + i, base + half + i] = 1.0
    return R


def _consts():
    import ml_dtypes
    ones = np.ones((128, 128), np.float32)
    bd = np.zeros((128, 128), np.float32)
    bd[0:64, 0:64] = 1.0
    bd[64:128, 64:128] = 1.0
    r_diff = _rot_matrix([(0, 32), (64, 32)])
    r_gqa = _rot_matrix([(0, 16), (32, 16), (64, 16), (96, 16)])
    r_mla = _rot_matrix([(64, 16)])
    sel = np.zeros((128, 128), np.float32)
    for i in range(32):
        sel[i, 64 + i] = 1.0
    ones96 = np.zeros((128, 128), np.float32)
    ones96[0:96, 0:96] = 1.0
    cst = np.concatenate([ones, bd, r_diff, r_gqa, r_mla, sel, ones96], axis=1)
    return cst.astype(ml_dtypes.bfloat16), _rope_tables().astype(ml_dtypes.bfloat16)


def _layer_weights(inp, l):
    w_in = np.asarray(inp["w_in"][l], np.float32)
    cq, ckv, kr = w_in[:, 0:384], w_in[:, 384:640], w_in[:, 640:672]
    dq, dk, dv = w_in[:, 672:1184], w_in[:, 1184:1696], w_in[:, 1696:2208]
    gq, gk, gv = w_in[:, 2208:2592], w_in[:, 2592:2720], w_in[:, 2720:2848]
    w_o = np.asarray(inp["w_o"][l], np.float32)
    w_uq = np.asarray(inp["mla_w_uq"][l], np.float32)
    w_ukv = np.asarray(inp["mla_w_ukv"][l], np.float32)
    w_gu = np.asarray(inp["w_gate_up"][l], np.float32)
    w_dn = np.asarray(inp["w_down"][l], np.float32)
    parts = {}
    for h in range(4):
        sl = slice(h * 128, (h + 1) * 128)
        parts["dw%d" % h] = _kc_layout(np.concatenate([dq[:, sl], dk[:, sl], dv[:, sl]], axis=1))
    wod = w_o[384:896].reshape(4, 128, 1024).transpose(1, 0, 2)
    parts["woda"] = np.ascontiguousarray(wod[:, :, 0:512]).reshape(128, 4 * 512)
    parts["wodb"] = np.ascontiguousarray(wod[:, :, 512:1024]).reshape(128, 4 * 512)
    parts["gwkv"] = _kc_layout(np.concatenate([gk[:, 0:64], gk[:, 0:64], gk[:, 64:128], gk[:, 64:128], gv], axis=1))
    parts["gwq"] = _kc_layout(gq)
    parts["wog"] = _kc_layout(w_o[896:1280])
    parts["mcq"] = _kc_layout(cq)
    parts["mckv"] = _kc_layout(ckv)
    krp = np.zeros((1024, 128), np.float32)
    krp[:, 0:32] = kr
    parts["mkr"] = _kc_layout(krp)
    wuqp = np.zeros((384, 6, 128), np.float32)
    wukk = np.zeros((256, 6, 128), np.float32)
    wukv = np.zeros((256, 6, 64), np.float32)
    for h in range(6):
        wuqp[:, h, 0:96] = w_uq[:, h * 96:(h + 1) * 96]
        wukk[:, h, 0:64] = w_ukv[:, h * 128:h * 128 + 64]
        wukv[:, h, :] = w_ukv[:, h * 128 + 64:(h + 1) * 128]
    parts["mlaw"] = np.concatenate([_kc_layout(wuqp.reshape(384, 768)), _kc_layout(wukk.reshape(256, 768)),
                                    _kc_layout(wukv.reshape(256, 384))], axis=1)
    parts["wom"] = _kc_layout(w_o[0:384])
    for j in range(NJ):
        parts["wgu%d" % j] = _kc_layout(np.concatenate([w_gu[:, j * 128:(j + 1) * 128],
                                                         w_gu[:, FF + j * 128:FF + (j + 1) * 128]], axis=1))
    for dc in range(8):
        parts["wd%d" % dc] = _kc_layout(w_dn[:, dc * 128:(dc + 1) * 128])
    W = np.empty((128, TOTW), np.float32)
    for n, c in BLOCKS:
        o, cc = BOFF[n]
        assert parts[n].shape == (128, cc), (n, parts[n].shape, cc)
        W[:, o:o + cc] = parts[n]
    return W


def _layer_vecs(inp, l):
    V = np.zeros((128, NV), np.float32)
    g = lambda name: np.asarray(inp[name][l], np.float32)
    V[:, 0:8] = g("attn_norm").reshape(8, 128).T
    V[:, 8:16] = g("ffn_norm").reshape(8, 128).T
    V[:, 16:19] = g("mla_q_norm").reshape(3, 128).T
    V[:, 19:21] = g("mla_kv_norm").reshape(2, 128).T
    V[0:96, 21] = g("mla_q_gain")
    V[0:96, 22] = g("mla_k_gain")
    V[:, 23] = np.tile(g("diff_q_gain"), 2)
    V[:, 24] = np.tile(g("diff_k_gain"), 2)
    V[:, 25] = g("diff_out_gain")
    V[:, 26] = np.tile(g("gqa_q_gain"), 2)
    V[:, 27] = np.tile(g("gqa_k_gain"), 2)
    V[:, 28:92] = g("diff_lq1")[None, :]
    V[:, 92:156] = g("diff_lk1")[None, :]
    V[:, 156:220] = g("diff_lq2")[None, :]
    V[:, 220:284] = g("diff_lk2")[None, :]
    return V


def _x_to_dev(xb):
    return np.ascontiguousarray(xb.T.reshape(KC, 128, S).transpose(1, 0, 2)).reshape(128, KC * S)


def _x_from_dev(yT):
    return np.ascontiguousarray(yT.reshape(128, KC, S).transpose(2, 1, 0)).reshape(S, D)


_PROG_CACHE = {}


def kernel(**inputs):
    x = np.asarray(inputs["x"], np.float32)
    B = x.shape[0]
    cst, tab = _consts()
    Ws = [_layer_weights(inputs, l) for l in range(DEPTH)]
    Vs = [_layer_vecs(inputs, l) for l in range(DEPTH)]
    if "nc" not in _PROG_CACHE:
        _PROG_CACHE["nc"] = build_program(DEPTH)
    nc = _PROG_CACHE["nc"]
    in_maps = []
    for b in range(B):
        m = {"xT": _x_to_dev(x[b]), "tab": tab, "cst": cst}
        for l in range(DEPTH):
            m["w%d" % l] = Ws[l]
            m["v%d" % l] = Vs[l]
        in_maps.append(m)
    res = run_bass_kernel_spmd(nc, in_maps, core_ids=list(range(B)))
    out = np.stack([_x_from_dev(np.asarray(r["yT"], np.float32)) for r in res.results], axis=0)
    return out.astype(np.float32)
```

```python
import math
import numpy as np
import concourse.bass as bass
import concourse.mybir as mybir
from concourse.bass_utils import run_bass_kernel_spmd

F32 = mybir.dt.float32
BF16 = mybir.dt.bfloat16
AF = mybir.ActivationFunctionType
ALU = mybir.AluOpType

ENGS = ("pe", "act", "dve", "pool", "sp")
STRICT_SAME_ENGINE = True


class View:
    __slots__ = ("ap", "root", "p0", "p1", "c0", "c1")

    def __init__(self, ap, root, p0, p1, c0, c1):
        self.ap, self.root, self.p0, self.p1, self.c0, self.c1 = ap, root, p0, p1, c0, c1


class Buf:
    def __init__(self, ap2d, esz, root, base_bytes=0):
        self.ap, self.esz, self.root, self.base = ap2d, esz, root, base_bytes

    def v(self, c0, c1, p0=0, p1=128):
        return View(self.ap[p0:p1, c0:c1], self.root, p0, p1,
                    self.base + c0 * self.esz, self.base + c1 * self.esz)

    def sub(self, c0, c1):
        return Buf(self.ap[:, c0:c1], self.esz, self.root, self.base + c0 * self.esz)


def _ov(a, v):
    return a[0] < v.p1 and v.p0 < a[1] and a[2] < v.c1 and v.c0 < a[3]


def _inside(a, v):
    return v.p0 <= a[0] and a[1] <= v.p1 and v.c0 <= a[2] and a[3] <= v.c1


class Op:
    __slots__ = ("eng", "fn", "idx", "deps", "dma_key", "signal", "count")


class Prog:
    def __init__(self, psum_roots=("ps",), bank_bytes=2048):
        self.per = {e: [] for e in ENGS}
        self.w = {}
        self.r = {}
        self.nops = 0
        self.psum_roots = set(psum_roots)
        self.bank_bytes = bank_bytes

    def _norm(self, v):
        if v.root not in self.psum_roots:
            return v
        bb = self.bank_bytes
        return View(v.ap, v.root, 0, 128, (v.c0 // bb) * bb, -(-v.c1 // bb) * bb)

    def add(self, eng, fn, reads=(), writes=(), dma_key=None):
        op = Op()
        op.eng, op.fn, op.dma_key, op.signal, op.count = eng, fn, dma_key, False, 0
        op.idx = len(self.per[eng])
        me = (eng, op.idx)
        deps = {}
        is_dma = dma_key is not None

        def dep(a, kind):
            key = (a[4], a[5])
            if key == me:
                return
            if kind != "raw" and deps.get(key) == "raw":
                return
            deps[key] = kind if deps.get(key) != "raw" else "raw"

        reads = [self._norm(v) for v in reads]
        writes = [self._norm(v) for v in writes]
        for v in reads:
            for a in self.w.get(v.root, ()):
                if _ov(a, v):
                    dep(a, "raw")
            if v.root in self.psum_roots:
                for a in self.r.get(v.root, ()):
                    if a[4] != eng and _ov(a, v):
                        dep(a, "rar")
        for v in writes:
            for a in self.w.get(v.root, ()):
                if _ov(a, v):
                    dep(a, "waw")
            for a in self.r.get(v.root, ()):
                if _ov(a, v):
                    dep(a, "war")
        for v in reads:
            rl = self.r.setdefault(v.root, [])
            if not is_dma:
                rl[:] = [a for a in rl if not (a[4] == eng and not a[6] and _inside(a, v))]
            rl.append((v.p0, v.p1, v.c0, v.c1, eng, op.idx, is_dma))
        for v in writes:
            wl = self.w.setdefault(v.root, [])
            rl = self.r.setdefault(v.root, [])
            wl[:] = [a for a in wl if not _inside(a, v)]
            rl[:] = [a for a in rl if not _inside(a, v)]
            wl.append((v.p0, v.p1, v.c0, v.c1, eng, op.idx, is_dma))
        op.deps = deps
        self.per[eng].append(op)
        self.nops += 1
        return op

    def emit(self, nc, block_ctx, sems, dma_sems):
        per = self.per
        waits = {e: [] for e in ENGS}
        for e in ENGS:
            seen = {}
            for op in per[e]:
                wl = []
                for (te, ti), kind in op.deps.items():
                    top = per[te][ti]
                    if top.dma_key is not None:
                        wl.append(top)
                        continue
                    if te == e:
                        if e == "pe" or (kind != "raw" and not STRICT_SAME_ENGINE):
                            continue
                    if seen.get(te, -1) >= ti:
                        continue
                    wl.append(top)
                best = {}
                out = []
                for top in wl:
                    if top.dma_key is not None:
                        out.append(top)
                    else:
                        if top.eng not in best or best[top.eng].idx < top.idx:
                            best[top.eng] = top
                for te, top in best.items():
                    seen[te] = top.idx
                    top.signal = True
                    out.append(top)
                waits[e].append(out)
        for e in ENGS:
            c = 0
            for op in per[e]:
                if op.dma_key is None and op.signal:
                    c += 1
                    op.count = c
        dcount = {}
        for e in ENGS:
            for op in per[e]:
                if op.dma_key is not None:
                    dcount[op.dma_key] = dcount.get(op.dma_key, 0) + 16
                    op.count = dcount[op.dma_key]
        self.stats = {e: (len(per[e]), sum(1 for o in per[e] if o.signal),
                          sum(len(w) for w in waits[e])) for e in ENGS}

        def run(e, handle):
            dseen = {}
            for op, wl in zip(per[e], waits[e]):
                for top in wl:
                    if top.dma_key is not None:
                        if dseen.get(top.dma_key, 0) >= top.count:
                            continue
                        dseen[top.dma_key] = top.count
                        handle.wait_ge(dma_sems[top.dma_key], top.count)
                    else:
                        handle.wait_ge(sems[top.eng], top.count)
                ins = op.fn(handle)
                if op.dma_key is not None:
                    ins.then_inc(dma_sems[op.dma_key], 16)
                elif op.signal:
                    ins.then_inc(sems[e], 1)

        @block_ctx.tensor
        def _(h):
            run("pe", h)

        @block_ctx.scalar
        def _(h):
            run("act", h)

        @block_ctx.vector
        def _(h):
            run("dve", h)

        @block_ctx.gpsimd
        def _(h):
            run("pool", h)

        @block_ctx.sync
        def _(h):
            run("sp", h)
            for k, c in dcount.items():
                h.wait_ge(dma_sems[k], c)


class K:
    def __init__(self, prog):
        self.p = prog

    def mm(self, out, lhsT, rhs, start=True, stop=True):
        self.p.add("pe", lambda e: e.matmul(out.ap, lhsT.ap, rhs.ap, start=start, stop=stop, skip_group_check=True),
                   reads=(lhsT, rhs), writes=(out,))

    def act(self, out, in_, func, scale=1.0, bias=0.0):
        reads = [in_]
        sc = scale
        bi = bias
        if isinstance(scale, View):
            reads.append(scale)
            sc = scale.ap
        if isinstance(bias, View):
            reads.append(bias)
            bi = bias.ap
        self.p.add("act", lambda e: e.activation(out.ap, in_.ap, func, bias=bi, scale=sc),
                   reads=reads, writes=(out,))

    def tt(self, out, in0, in1, op, eng="dve"):
        self.p.add(eng, lambda e: e.tensor_tensor(out.ap, in0.ap, in1.ap, op),
                   reads=(in0, in1), writes=(out,))

    def ts(self, out, in0, s1, s2, op0, op1=None, eng="dve"):
        reads = [in0]
        a1, a2 = s1, s2
        if isinstance(s1, View):
            reads.append(s1)
            a1 = s1.ap
        if isinstance(s2, View):
            reads.append(s2)
            a2 = s2.ap
        if op1 is None:
            self.p.add(eng, lambda e: e.tensor_scalar(out.ap, in0.ap, a1, None, op0),
                       reads=reads, writes=(out,))
        else:
            self.p.add(eng, lambda e: e.tensor_scalar(out.ap, in0.ap, a1, a2, op0, op1),
                       reads=reads, writes=(out,))

    def stt(self, out, in0, scalar, in1, op0, op1, eng="dve"):
        reads = [in0, in1]
        s = scalar
        if isinstance(scalar, View):
            reads.append(scalar)
            s = scalar.ap
        self.p.add(eng, lambda e: e.scalar_tensor_tensor(out.ap, in0.ap, s, in1.ap, op0, op1),
                   reads=reads, writes=(out,))

    def copy(self, out, in_, eng="dve"):
        self.p.add(eng, lambda e: e.tensor_copy(out.ap, in_.ap), reads=(in_,), writes=(out,))

    def recip(self, out, in_):
        self.p.add("dve", lambda e: e.reciprocal(out.ap, in_.ap), reads=(in_,), writes=(out,))

    def memset(self, out, val, eng="dve"):
        self.p.add(eng, lambda e: e.memset(out.ap, val), reads=(), writes=(out,))

    def reduce_sum(self, out, in_):
        self.p.add("dve", lambda e: e.reduce_sum(out.ap, in_.ap, mybir.AxisListType.X),
                   reads=(in_,), writes=(out,))

    def dma_in(self, queue, key, out, dram_ap):
        self.p.add(queue, lambda e: e.dma_start(out=out.ap, in_=dram_ap), reads=(), writes=(out,),
                   dma_key=key)

    def dma_out(self, queue, key, dram_ap, in_):
        self.p.add(queue, lambda e: e.dma_start(out=dram_ap, in_=in_.ap), reads=(in_,), writes=(),
                   dma_key=key)


import collections
from contextlib import ExitStack

S = 2048
D = 1024
TT = 512
NT = 4
KC = 8
DEPTH = 2
FF = 2816
NJ = 22
EPS = 1e-6
THETA = 10000.0
NSLOT = 3
SLOT = 3072
NV = 284
NCST = 7
A_MIX = 0
A_EXP = 8192
A_QT = A_EXP + 3 * 1024
A_KV = A_QT + 4 * 512
ARENA = A_KV + 14336
MLAW_N = 3 * 768 + 2 * 768 + 2 * 384
POOL_MLA = True


def layer_blocks():
    b = []
    for h in range(4):
        b.append(("dw%d" % h, 8 * 384))
    b += [("gwkv", 8 * 384), ("gwq", 8 * 384), ("woda", 4 * 512), ("wodb", 4 * 512), ("wog", 3 * 1024),
          ("mcq", 8 * 384), ("mckv", 8 * 256), ("mkr", 8 * 32), ("mlaw", MLAW_N),
          ("wom", 3 * 1024)]
    for j in range(NJ):
        b.append(("wgu%d" % j, 8 * 256))
    for dc in range(8):
        b.append(("wd%d" % dc, NJ * 128))
    off = {}
    o = 0
    for n, c in b:
        off[n] = (o, c)
        o += c
    return b, off, o


BLOCKS, BOFF, TOTW = layer_blocks()


def lambda_init(l):
    return 0.8 - 0.6 * math.exp(-0.3 * l)


class WStream:
    def __init__(self, k, wsl, w_d, seq):
        self.k, self.wsl, self.w_d, self.seq = k, wsl, w_d, seq
        self.issued = 0
        self.live = {}
        self.free = list(range(NSLOT))
        for _ in range(NSLOT):
            self._issue()

    def _issue(self):
        if self.issued >= len(self.seq) or not self.free:
            return
        l, name = self.seq[self.issued]
        slot = self.free.pop(0)
        off, n = BOFF[name]
        buf = self.wsl.sub(slot * SLOT, slot * SLOT + n)
        self.k.dma_in("pool", "ws%d" % slot, buf.v(0, n), self.w_d[l][:, off:off + n])
        self.live[self.issued] = (slot, buf, l, name)
        self.issued += 1

    def get(self, l, name):
        for i, (slot, buf, ll, nn) in self.live.items():
            if ll == l and nn == name:
                return buf
        raise RuntimeError("weight block %s/%d not resident; live=%s" % (name, l, [(v[2], v[3]) for v in self.live.values()]))

    def done(self, l, name):
        for i, (slot, buf, ll, nn) in list(self.live.items()):
            if ll == l and nn == name:
                del self.live[i]
                self.free.append(slot)
                self._issue()
                return
        raise RuntimeError("done: %s not live" % name)


class _Stop(Exception):
    pass


def build_program(n_layers=DEPTH, first_layer=0, stop_after=None):
    nc = bass.Bass("TRN2", target_bir_lowering=False)
    xT_d = nc.dram_tensor("xT", [128, KC * S], F32, kind="ExternalInput").ap()
    yT_d = nc.dram_tensor("yT", [128, KC * S], F32, kind="ExternalOutput").ap()
    w_d = [nc.dram_tensor("w%d" % l, [128, TOTW], F32, kind="ExternalInput").ap() for l in range(n_layers)]
    v_d = [nc.dram_tensor("v%d" % l, [128, NV], F32, kind="ExternalInput").ap() for l in range(n_layers)]
    tab_d = nc.dram_tensor("tab", [128, 6 * S], BF16, kind="ExternalInput").ap()
    cst_d = nc.dram_tensor("cst", [128, NCST * 128], BF16, kind="ExternalInput").ap()

    with ExitStack() as es:
        def sb(name, cols, dt):
            return es.enter_context(nc.sbuf_tensor(name, [128, cols], dt))
        xT_t = sb("xT_s", KC * S, F32)
        xn_t = sb("xn_s", KC * S, BF16)
        tab_t = sb("tab_s", 2 * 2 * S, BF16)
        wsl_t = sb("wsl_s", NSLOT * SLOT, BF16)
        ar_t = sb("arena_s", ARENA, BF16)
        cst_t = sb("cst_s", NCST * 128, BF16)
        vec_t = sb("vec_s", n_layers * NV, F32)
        scb_t = sb("scb_s", 4 * 512, BF16)
        scf_t = sb("scf_s", 7 * 512, F32)
        sm_t = sb("sm_s", 16, F32)
        ps_t = es.enter_context(nc.psum_tensor("ps", [128, 8 * 512], F32))
        dma_keys = ["ws%d" % i for i in range(NSLOT)] + ["xin%d" % i for i in range(KC)] + ["xinb%d" % i for i in range(KC)] + \
                   ["yout", "tab0", "tab1", "cst", "mlaw"] + ["vec%d" % i for i in range(n_layers)]
        sems = {e: es.enter_context(nc.semaphore("sem_" + e)) for e in ENGS}
        dsems = {kk: es.enter_context(nc.semaphore("dsem_" + kk)) for kk in dma_keys}
        block = es.enter_context(nc.Block())

        P = Prog()
        k = K(P)
        xT = Buf(xT_t[:], 4, "xT")
        xn = Buf(xn_t[:], 2, "xn")
        tab = Buf(tab_t[:], 2, "tab")
        wsl = Buf(wsl_t[:], 2, "wsl")
        ar = Buf(ar_t[:], 2, "arena")
        cst = Buf(cst_t[:], 2, "cst")
        vec = Buf(vec_t[:], 4, "vec")
        scb = Buf(scb_t[:], 2, "scb")
        scf = Buf(scf_t[:], 4, "scf")
        sm = Buf(sm_t[:], 4, "sm")
        ps = Buf(ps_t[:], 4, "ps")

        def bank(i, p0=0, p1=128, c0=0, c1=512):
            return ps.v(i * 512 + c0, i * 512 + c1, p0, p1)

        ONES = cst.sub(0, 128)
        BD64 = cst.sub(128, 256)
        R_DIFF = cst.sub(256, 384)
        R_GQA = cst.sub(384, 512)
        R_MLA = cst.sub(512, 640)
        SELKR = cst.sub(640, 768)
        ONES96 = cst.sub(768, 896)

        class SS:
            def __init__(self, i):
                self.bfA = scb.sub((2 * i) * 512, (2 * i + 1) * 512)
                self.bfB = scb.sub((2 * i + 1) * 512, (2 * i + 2) * 512)
                self.f0 = scf.sub((3 * i) * 512, (3 * i + 1) * 512)
                self.f1 = scf.sub((3 * i + 1) * 512, (3 * i + 2) * 512)
                self.f2 = scf.sub((3 * i + 2) * 512, (3 * i + 3) * 512)
        SSETS = [SS(0), SS(1)]
        EF0 = scf.sub(6 * 512, 7 * 512)
        ss_ctr = [0]

        def next_ss():
            ss_ctr[0] += 1
            return SSETS[ss_ctr[0] % 2]

        bgq = collections.deque()

        def place(stages, start=0, stride=1):
            for s_, fn in enumerate(stages):
                idx = start + s_ * stride
                while len(bgq) <= idx:
                    bgq.append([])
                bgq[idx].append(fn)

        def tick(n=1):
            for _ in range(n):
                if bgq:
                    for fn in bgq.popleft():
                        fn()

        def drain():
            while bgq:
                tick()

        def run_interleaved(tasks):
            tasks = [list(t) for t in tasks]
            m = max(len(t) for t in tasks)
            for s_ in range(m):
                for t in tasks:
                    if s_ < len(t):
                        t[s_]()

        k.dma_in("sp", "cst", cst.v(0, NCST * 128), cst_d)
        for l in range(n_layers):
            k.dma_in("sp", "vec%d" % l, vec.v(l * NV, (l + 1) * NV), v_d[l])
        for c in range(KC):
            k.dma_in("sp", "xin%d" % c, xT.v(c * S, c * S + TT), xT_d[:, c * S:c * S + TT])
        for c in range(KC):
            k.dma_in("sp", "xinb%d" % c, xT.v(c * S + TT, (c + 1) * S), xT_d[:, c * S + TT:(c + 1) * S])
        seq = []
        for l in range(n_layers):
            for n, _ in BLOCKS:
                if n == "mlaw" or n.startswith("wgu") or n.startswith("wd"):
                    continue
                seq.append((l, n))
            for half in range(2):
                for j in range(NJ):
                    seq.append((l, "wgu%d" % j))
                for dc in range(8):
                    seq.append((l, "wd%d" % dc))
        ws = WStream(k, wsl, w_d, seq)

        tab_phase = [0]

        def load_tab(kind):
            i = tab_phase[0] % 2
            tab_phase[0] += 1
            base = i * 2 * S
            k.dma_in("sp", "tab%d" % i, tab.v(base, base + 2 * S), tab_d[:, kind * 2 * S:(kind + 1) * 2 * S])
            return tab.sub(base, base + S), tab.sub(base + S, base + 2 * S)

        def vcol(l, c, p0=0, p1=128):
            return vec.v(l * NV + c, l * NV + c + 1, p0, p1)

        def rstd_from(dst, ms_view, inv_d):
            k.act(dst, ms_view, AF.Ln, scale=inv_d, bias=EPS)
            k.act(dst, dst, AF.Exp, scale=-0.5)

        def rmsnorm_x(l, gbase):
            for t in range(NT):
                Sx = next_ss()
                b = t % 2
                sq = [Sx.bfA, Sx.bfB]
                for c in range(KC):
                    k.act(sq[c % 2].v(0, TT), xT.v(c * S + t * TT, c * S + (t + 1) * TT), AF.Square)
                    k.mm(bank(b), ONES.v(0, 128), sq[c % 2].v(0, TT), start=(c == 0), stop=(c == KC - 1))
                rstd_from(Sx.f1.v(0, TT), bank(b), 1.0 / D)
                for c in range(KC):
                    k.stt(xn.v(c * S + t * TT, c * S + (t + 1) * TT), xT.v(c * S + t * TT, c * S + (t + 1) * TT),
                          vcol(l, gbase + c), Sx.f1.v(0, TT), ALU.mult, ALU.mult)

        def chain_stages(X, Y, proj_fn, gcol, cosb, sinb, t, Rm, OM, inv_d, dsts, after=None, pool=False):
            st = {}

            def A():
                proj_fn(X)

            def B():
                Sx = st["S"] = next_ss()
                k.act(Sx.bfA.v(0, TT), bank(X), AF.Square)
                k.ts(Sx.bfB.v(0, TT), bank(X), gcol, None, ALU.mult)

            def C():
                Sx = st["S"]
                k.mm(bank(Y), OM.v(0, 128), Sx.bfA.v(0, TT))
                k.mm(bank(X), Rm.v(0, 128), Sx.bfB.v(0, TT))

            def Dd():
                Sx = st["S"]
                rstd_from(Sx.f1.v(0, TT), bank(Y), inv_d)
                pe_ = "pool" if pool else "dve"
                k.tt(Sx.f0.v(0, TT), Sx.bfB.v(0, TT), cosb.v(t * TT, (t + 1) * TT), ALU.mult, eng=pe_)
                k.tt(Sx.f2.v(0, TT), bank(X), sinb.v(t * TT, (t + 1) * TT), ALU.mult)
                k.tt(Sx.f0.v(0, TT), Sx.f0.v(0, TT), Sx.f2.v(0, TT), ALU.add, eng=pe_)
                for (db, c0, p0, p1) in dsts:
                    k.tt(db.v(c0, c0 + TT, p0, p1), Sx.f0.v(0, TT, p0, p1), Sx.f1.v(0, TT, p0, p1), ALU.mult)
                if after is not None:
                    after()
            return [A, B, C, Dd]

        def proj(dst_bank, wblk, wstride, wc0, M, t):
            for kc in range(KC):
                k.mm(bank(dst_bank, 0, M), wblk.v(kc * wstride + wc0, kc * wstride + wc0 + M),
                     xn.v(kc * S + t * TT, kc * S + (t + 1) * TT), start=(kc == 0), stop=(kc == KC - 1))

        EXPT = [ar.sub(A_EXP + i * 1024, A_EXP + (i + 1) * 1024) for i in range(3)]
        exp_ctr = [0]
        QT = [ar.sub(A_QT + i * 512, A_QT + (i + 1) * 512) for i in range(4)]
        MIX = [ar.sub(A_MIX + i * S, A_MIX + (i + 1) * S) for i in range(4)]
        KVR = ar.sub(A_KV, A_KV + 14336)

        def attn_pairs(npairs, score_fn, pv_fn, scale, nt=1):
            def S_(i):
                pb = 2 * (i % 2)
                score_fn(i, pb, pb + 1)
            S_(0)
            if npairs > 1:
                S_(1)
            for i in range(npairs):
                pb = 2 * (i % 2)
                eb = EXPT[exp_ctr[0] % 3]
                exp_ctr[0] += 1
                k.act(eb.v(0, 1024), ps.v(pb * 512, (pb + 2) * 512), AF.Exp, scale=scale)
                tick(nt)
                if i + 2 < npairs:
                    S_(i + 2)
                pv_fn(i, eb)

        def wo_stage(l, nch, blocks, per_block_dc):
            cnt = 0
            for bi, bn in enumerate(blocks):
                blk = ws.get(l, bn)
                width = per_block_dc * 128
                for dci in range(per_block_dc):
                    dc = bi * per_block_dc + dci
                    for t in range(NT):
                        b = cnt % 4
                        cnt += 1
                        for c in range(nch):
                            k.mm(bank(b), blk.v(c * width + dci * 128, c * width + (dci + 1) * 128),
                                 MIX[c].v(t * TT, (t + 1) * TT), start=(c == 0), stop=(c == nch - 1))
                        xv = xT.v(dc * S + t * TT, dc * S + (t + 1) * TT)
                        k.tt(xv, xv, bank(b), ALU.add)
                ws.done(l, bn)

        def phase(name):
            if stop_after == name:
                raise _Stop()

        def layer_body(l):
            lam0 = lambda_init(l + first_layer)
            Sx = next_ss()
            for i, (ca, cb) in enumerate(((28, 92), (156, 220))):
                k.tt(Sx.f0.v(0, 64), vec.v(l * NV + ca, l * NV + ca + 64), vec.v(l * NV + cb, l * NV + cb + 64), ALU.mult)
                k.reduce_sum(sm.v(i, i + 1), Sx.f0.v(0, 64))
                k.act(sm.v(2 + i, 3 + i), sm.v(i, i + 1), AF.Exp)
            k.tt(sm.v(4, 5), sm.v(2, 3), sm.v(3, 4), ALU.subtract)
            k.ts(sm.v(5, 6), sm.v(4, 5), -1.0, -lam0, ALU.mult, ALU.add)
            k.ts(sm.v(6, 7), vcol(l, 25), 1.0 - lam0, None, ALU.mult)
            NLAM = sm.v(5, 6)
            GSC = sm.v(6, 7)

            rmsnorm_x(l, 0)
            k.memset(ar.v(A_QT, A_QT + 4 * 512), 0.0, eng="pool")
            phase('norm')

            dcos, dsin = load_tab(0)
            gcos, gsin = load_tab(1)
            KD = [KVR.sub(0, 2048), KVR.sub(2048, 4096)]
            VG = [KVR.sub(8192, 8192 + 3072), KVR.sub(8192 + 3072, 8192 + 6144)]
            k.memset(VG[0].v(0, 3072), 1.0, eng="pool")
            k.memset(VG[1].v(0, 3072), 1.0, eng="pool")

            def dkv(h):
                par = h % 2
                return KVR.sub(par * 4096, par * 4096 + 2048), KVR.sub(par * 4096 + 2048, par * 4096 + 4096)

            def diff_k_stages(h, t, X, Y):
                kT, vv = dkv(h)
                return chain_stages(X, Y, lambda b_: proj(b_, ws.get(l, "dw%d" % h), 384, 128, 128, t),
                                    vcol(l, 24), dcos, dsin, t, R_DIFF, BD64, 1.0 / 64, [(kT, t * TT, 0, 128)])

            def diff_v_stages(h, t, Z):
                kT, vv = dkv(h)

                def A():
                    blk = ws.get(l, "dw%d" % h)
                    for sc in range(4):
                        tok0 = t * TT + sc * 128
                        for kc in range(KC):
                            k.mm(bank(Z, 0, 128, sc * 128, (sc + 1) * 128), xn.v(kc * S + tok0, kc * S + tok0 + 128),
                                 blk.v(kc * 384 + 256, kc * 384 + 384), start=(kc == 0), stop=(kc == KC - 1))

                def B():
                    k.copy(vv.v(t * TT, (t + 1) * TT), bank(Z))
                return [A, B]

            def diff_q_stages(h, t, qi, X, Y, after=None):
                return chain_stages(X, Y, lambda b_: proj(b_, ws.get(l, "dw%d" % h), 384, 0, 128, t),
                                    vcol(l, 23), dcos, dsin, t, R_DIFF, BD64, 1.0 / 64,
                                    [(QT[2 * qi], 0, 0, 64), (QT[2 * qi + 1], 0, 64, 128)], after=after)

            def gqa_k_stages(t, g, X, Y):
                return chain_stages(X, Y, lambda b_: proj(b_, ws.get(l, "gwkv"), 384, g * 128, 128, t),
                                    vcol(l, 27), gcos, gsin, t, R_GQA, BD64, 1.0 / 64, [(KD[g], t * TT, 0, 128)])

            def gqa_v_stages(t, Z, last=False):
                def A():
                    blk = ws.get(l, "gwkv")
                    for sc in range(4):
                        tok0 = t * TT + sc * 128
                        for kc in range(KC):
                            k.mm(bank(Z, 0, 128, sc * 128, (sc + 1) * 128), xn.v(kc * S + tok0, kc * S + tok0 + 128),
                                 blk.v(kc * 384 + 256, kc * 384 + 384), start=(kc == 0), stop=(kc == KC - 1))
                    if last:
                        ws.done(l, "gwkv")

                def B():
                    for sc in range(4):
                        for g in range(2):
                            c0 = (t * 4 + sc) * 192 + 64
                            k.copy(VG[g].v(c0, c0 + 64), bank(Z, 0, 128, sc * 128 + g * 64, sc * 128 + (g + 1) * 64))
                return [A, B]

            def gqa_q_stages(c, t, qi, X, Y, after=None):
                return chain_stages(X, Y, lambda b_: proj(b_, ws.get(l, "gwq"), 384, c * 128, 128, t),
                                    vcol(l, 26), gcos, gsin, t, R_GQA, BD64, 1.0 / 64,
                                    [(QT[2 * qi], 0, 0, 64), (QT[2 * qi + 1], 0, 64, 128)], after=after)

            EXP5 = [ar.sub(A_EXP + i * 512, A_EXP + (i + 1) * 512) for i in range(6)]

            def diff_unit(h, t8, ui):
                kT, vv = dkv(h)
                t, half = t8 // 2, t8 % 2
                pq = (h * 4 + t) % 2
                qA, qB = QT[2 * pq], QT[2 * pq + 1]
                c0 = half * 256
                OB, SM = (2, 3) if ui % 2 == 0 else (4, 5)

                def S_(i):
                    b = i % 2
                    k.mm(bank(b, 0, 128, 0, 256), kT.v(i * 128, (i + 1) * 128), qA.v(c0, c0 + 256), start=True, stop=True)
                    k.mm(bank(b, 0, 128, 256, 512), kT.v(i * 128, (i + 1) * 128), qB.v(c0, c0 + 256), start=False, stop=True)
                S_(0)
                S_(1)
                for i in range(16):
                    eb = EXP5[exp_ctr[0] % 6]
                    exp_ctr[0] += 1
                    k.act(eb.v(0, 512), bank(i % 2), AF.Exp, scale=0.125)
                    first, last = (i == 0), (i == 15)
                    tick()
                    if i + 2 < 16:
                        S_(i + 2)
                    k.mm(bank(OB, 0, 128, 0, 256), vv.v(i * 128, (i + 1) * 128), eb.v(0, 256), start=first, stop=last)
                    k.mm(bank(OB, 0, 128, 256, 512), vv.v(i * 128, (i + 1) * 128), eb.v(256, 512), start=False, stop=last)
                    k.mm(bank(SM, 0, 128, 0, 256), ONES.v(0, 128), eb.v(0, 256), start=first, stop=last)
                    k.mm(bank(SM, 0, 128, 256, 512), ONES.v(0, 128), eb.v(256, 512), start=False, stop=last)
                Sx = next_ss()
                k.recip(EF0.v(0, 512), bank(SM))
                k.tt(Sx.f0.v(0, 256), bank(OB, 0, 128, 0, 256), EF0.v(0, 256), ALU.mult)
                k.tt(Sx.f2.v(0, 256), bank(OB, 0, 128, 256, 512), EF0.v(256, 512), ALU.mult)
                k.stt(Sx.f0.v(0, 256), Sx.f2.v(0, 256), NLAM, Sx.f0.v(0, 256), ALU.mult, ALU.add)

                def Esq():
                    k.tt(Sx.bfA.v(0, 256), Sx.f0.v(0, 256), Sx.f0.v(0, 256), ALU.mult)

                def E2():
                    k.mm(bank(SM, 0, 128, 0, 256), ONES.v(0, 128), Sx.bfA.v(0, 256))

                def E3():
                    rstd_from(Sx.f1.v(0, 256), bank(SM, 0, 128, 0, 256), 1.0 / 128)
                    k.stt(MIX[h].v(t8 * 256, (t8 + 1) * 256), Sx.f0.v(0, 256), GSC, Sx.f1.v(0, 256), ALU.mult, ALU.mult)
                return [Esq, E2, E3]

            run_interleaved([diff_k_stages(0, 0, 0, 1), diff_k_stages(0, 1, 2, 3)])
            run_interleaved([diff_k_stages(0, 2, 0, 1), diff_k_stages(0, 3, 2, 3), diff_v_stages(0, 0, 4), diff_v_stages(0, 1, 5)])
            run_interleaved([diff_q_stages(0, 0, 0, 6, 7), diff_v_stages(0, 2, 4), diff_v_stages(0, 3, 5)])
            phase('diffpro')
            tiles = [(h, t) for h in range(4) for t in range(NT)]
            ui = 0
            epi = None
            for ti, (h, t) in enumerate(tiles):
                pq_next = (ti + 1) % 2
                for half in range(2):
                    if epi is not None:
                        place(epi, 8, 2)
                    if half == 0:
                        if ti + 1 < len(tiles):
                            nh, nt_ = tiles[ti + 1]
                            place(diff_q_stages(nh, nt_, pq_next, 6, 7), 0, 2)
                        else:
                            place(gqa_q_stages(0, 0, pq_next, 6, 7), 0, 2)
                        if h == 3:
                            place(gqa_k_stages(t, 1, 6, 7), 8, 2)
                    else:
                        if h + 1 < 4:
                            place(diff_k_stages(h + 1, t, 6, 7), 0, 2)
                            place(diff_v_stages(h + 1, t, 7), 8, 2)
                        else:
                            place(gqa_k_stages(t, 0, 6, 7), 0, 2)
                            place(gqa_v_stages(t, 7, last=(t == NT - 1)), 8, 2)
                    epi = diff_unit(h, t * 2 + half, ui)
                    ui += 1
                    if ui == 1:
                        phase('diffu0')
                if t == NT - 1 and h < 3:
                    ws.done(l, "dw%d" % h)
            place(epi, 0, 2)
            drain()
            ws.done(l, "dw3")
            wo_stage(l, 4, ["woda", "wodb"], 4)
            phase('diff')

            mcos, msin = load_tab(2)

            def gqa_unit(c, t, qi, aA, aB):
                ge, go = (2 * c) // 3, (2 * c + 1) // 3
                qA, qB = QT[2 * qi], QT[2 * qi + 1]

                def sf(i, bA, bB):
                    k.mm(bank(bA), KD[ge].v(i * 128, (i + 1) * 128), qA.v(0, TT))
                    k.mm(bank(bB), KD[go].v(i * 128, (i + 1) * 128), qB.v(0, TT))

                def pv(i, eb):
                    st, sp_ = (i == 0), (i == 15)
                    k.mm(bank(aA), VG[ge].v(i * 192 + 64, i * 192 + 192), eb.v(0, 512), start=st, stop=sp_)
                    k.mm(bank(aB), VG[go].v(i * 192, i * 192 + 128), eb.v(512, 1024), start=st, stop=sp_)
                attn_pairs(16, sf, pv, 0.125)
                k.recip(EF0.v(0, TT, 0, 64), bank(aA, 64, 128))
                k.tt(MIX[c].v(t * TT, (t + 1) * TT, 0, 64), bank(aA, 0, 64), EF0.v(0, TT, 0, 64), ALU.mult)
                k.recip(EF0.v(0, TT, 64, 128), bank(aB, 0, 64))
                k.tt(MIX[c].v(t * TT, (t + 1) * TT, 64, 128), bank(aB, 64, 128), EF0.v(0, TT, 64, 128), ALU.mult)

            units = [(c, t) for c in range(3) for t in range(NT)]
            for ui, (c, t) in enumerate(units):
                qi = (16 + ui) % 2
                aA, aB = (6, 7) if ui % 2 == 0 else (4, 5)
                oA, oB = (4, 5) if ui % 2 == 0 else (6, 7)
                if ui + 1 < len(units):
                    place(gqa_q_stages(units[ui + 1][0], units[ui + 1][1], (16 + ui + 1) % 2, oA, oB), 6, 2)
                gqa_unit(c, t, qi, aA, aB)
            drain()
            ws.done(l, "gwq")
            wo_stage(l, 3, ["wog"], 8)
            phase('gqa')

            CQN = [xn.sub(m * S, (m + 1) * S) for m in range(3)]
            CKVN = [xn.sub((3 + m) * S, (4 + m) * S) for m in range(2)]
            MLAW = xn.sub(5 * S, 5 * S + MLAW_N)
            KRT = MIX[3]
            bq, bkv, bkr = ws.get(l, "mcq"), ws.get(l, "mckv"), ws.get(l, "mkr")
            for t in range(NT):
                for m in range(3):
                    proj(m, bq, 384, m * 128, 128, t)
                for m in range(2):
                    proj(4 + m, bkv, 256, m * 128, 128, t)
                proj(6, bkr, 32, 0, 32, t)
                Sx = next_ss()
                for m in range(3):
                    k.act(Sx.bfA.v(0, TT), bank(m), AF.Square)
                    k.mm(bank(3), ONES.v(0, 128), Sx.bfA.v(0, TT), start=(m == 0), stop=(m == 2))
                for m in range(2):
                    k.act(Sx.bfB.v(0, TT), bank(4 + m), AF.Square)
                    k.mm(bank(7), ONES.v(0, 128), Sx.bfB.v(0, TT), start=(m == 0), stop=(m == 1))
                rstd_from(Sx.f0.v(0, TT), bank(3), 1.0 / 384)
                rstd_from(Sx.f1.v(0, TT), bank(7), 1.0 / 256)
                for m in range(3):
                    k.stt(CQN[m].v(t * TT, (t + 1) * TT), bank(m), vcol(l, 16 + m), Sx.f0.v(0, TT), ALU.mult, ALU.mult)
                for m in range(2):
                    k.stt(CKVN[m].v(t * TT, (t + 1) * TT), bank(4 + m), vcol(l, 19 + m), Sx.f1.v(0, TT), ALU.mult, ALU.mult)
                k.copy(KRT.v(t * TT, (t + 1) * TT, 0, 32), bank(6, 0, 32))
            ws.done(l, "mcq")
            ws.done(l, "mckv")
            ws.done(l, "mkr")
            moff, mn = BOFF["mlaw"]
            k.dma_in("pool", "mlaw", MLAW.v(0, MLAW_N), w_d[l][:, moff:moff + mn])
            WUQ = MLAW.sub(0, 2304)
            WUKK = MLAW.sub(2304, 2304 + 1536)
            WUKV = MLAW.sub(3840, 3840 + 768)

            def mkv(h):
                base = 4096 if h % 2 == 0 else 0
                return KVR.sub(base, base + 2048), KVR.sub(base + 2048, base + 4096)

            def mla_k_stages(h, t, X, Y):
                kT, va = mkv(h)

                def pj(b_):
                    if t == 0:
                        k.memset(va.v(0, 2048), 1.0, eng="pool")
                    for m in range(2):
                        k.mm(bank(b_), WUKK.v(m * 768 + h * 128, m * 768 + (h + 1) * 128), CKVN[m].v(t * TT, (t + 1) * TT),
                             start=(m == 0), stop=False)
                    k.mm(bank(b_), SELKR.v(0, 128, 0, 32), KRT.v(t * TT, (t + 1) * TT, 0, 32), start=False, stop=True)
                return chain_stages(X, Y, pj, vcol(l, 22), mcos, msin, t, R_MLA, ONES96, 1.0 / 96, [(kT, t * TT, 0, 128)], pool=POOL_MLA)

            def mla_v_stages(h, t, Z):
                kT, va = mkv(h)
                voff = 0 if h % 2 == 0 else 64

                def A():
                    for sc in range(4):
                        tok0 = t * TT + sc * 128
                        for m in range(2):
                            k.mm(bank(Z, 0, 128, sc * 64, (sc + 1) * 64), CKVN[m].v(tok0, tok0 + 128),
                                 WUKV.v(m * 384 + h * 64, m * 384 + (h + 1) * 64), start=(m == 0), stop=(m == 1))

                def B():
                    for sc in range(4):
                        c0 = (t * 4 + sc) * 128 + voff
                        k.copy(va.v(c0, c0 + 64), bank(Z, 0, 128, sc * 64, (sc + 1) * 64))
                return [A, B]

            def mla_q_stages(h, t, qi, X, Y):
                def pj(b_):
                    for m in range(3):
                        k.mm(bank(b_), WUQ.v(m * 768 + h * 128, m * 768 + (h + 1) * 128), CQN[m].v(t * TT, (t + 1) * TT),
                             start=(m == 0), stop=(m == 2))
                return chain_stages(X, Y, pj, vcol(l, 21), mcos, msin, t, R_MLA, ONES96, 1.0 / 96, [(QT[qi], 0, 0, 128)], pool=POOL_MLA)

            def mla_unit(h, t, qi, acc):
                kT, va = mkv(h)
                q = QT[qi]

                def sf(i, bA, bB):
                    k.mm(bank(bA), kT.v((2 * i) * 128, (2 * i + 1) * 128), q.v(0, TT))
                    k.mm(bank(bB), kT.v((2 * i + 1) * 128, (2 * i + 2) * 128), q.v(0, TT))

                def pv(i, eb):
                    k.mm(bank(acc), va.v((2 * i) * 128, (2 * i + 1) * 128), eb.v(0, 512), start=(i == 0), stop=False)
                    k.mm(bank(acc), va.v((2 * i + 1) * 128, (2 * i + 2) * 128), eb.v(512, 1024), start=False, stop=(i == 7))
                attn_pairs(8, sf, pv, 96 ** -0.5, nt=2)
                c = h // 2
                if h % 2 == 0:
                    k.recip(EF0.v(0, TT, 0, 64), bank(acc, 64, 128))
                    k.tt(MIX[c].v(t * TT, (t + 1) * TT, 0, 64), bank(acc, 0, 64), EF0.v(0, TT, 0, 64), ALU.mult)
                else:
                    k.recip(EF0.v(0, TT, 64, 128), bank(acc, 0, 64))
                    k.tt(MIX[c].v(t * TT, (t + 1) * TT, 64, 128), bank(acc, 64, 128), EF0.v(0, TT, 64, 128), ALU.mult)

            run_interleaved([mla_k_stages(0, 0, 0, 1), mla_k_stages(0, 1, 2, 3)])
            run_interleaved([mla_k_stages(0, 2, 0, 1), mla_k_stages(0, 3, 2, 3), mla_v_stages(0, 0, 4), mla_v_stages(0, 1, 5)])
            run_interleaved([mla_q_stages(0, 0, 0, 6, 7), mla_v_stages(0, 2, 4), mla_v_stages(0, 3, 5)])
            units = [(h, t) for h in range(6) for t in range(NT)]
            for ui, (h, t) in enumerate(units):
                acc = 4 + ui % 2
                idle = 5 - ui % 2
                if ui + 1 < len(units):
                    place(mla_q_stages(units[ui + 1][0], units[ui + 1][1], (ui + 1) % 2, 6, 7), 0, 2)
                if h + 1 < 6:
                    place(mla_k_stages(h + 1, t, idle, 7), 8, 2)
                    place(mla_v_stages(h + 1, t, 6), 11, 2)
                mla_unit(h, t, ui % 2, acc)
            drain()
            wo_stage(l, 3, ["wom"], 8)
            phase('mla')

            rmsnorm_x(l, 8)
            ACTT = ar.sub(0, NJ * 1024)
            for half in range(2):
                for j in range(NJ):
                    blk = ws.get(l, "wgu%d" % j)
                    for t2 in range(2):
                        t = half * 2 + t2
                        bG = (j % 2) * 4 + t2 * 2
                        bU = bG + 1
                        for kc in range(KC):
                            k.mm(bank(bG), blk.v(kc * 256, kc * 256 + 128), xn.v(kc * S + t * TT, kc * S + (t + 1) * TT),
                                 start=(kc == 0), stop=(kc == KC - 1))
                        for kc in range(KC):
                            k.mm(bank(bU), blk.v(kc * 256 + 128, kc * 256 + 256), xn.v(kc * S + t * TT, kc * S + (t + 1) * TT),
                                 start=(kc == 0), stop=(kc == KC - 1))
                        Sx = next_ss()
                        k.act(Sx.bfA.v(0, TT), bank(bG), AF.Silu)
                        k.tt(ACTT.v(j * 1024 + t2 * TT, j * 1024 + (t2 + 1) * TT), Sx.bfA.v(0, TT), bank(bU), ALU.mult)
                    ws.done(l, "wgu%d" % j)
                cnt = 0
                for dc in range(8):
                    blk = ws.get(l, "wd%d" % dc)
                    for t2 in range(2):
                        t = half * 2 + t2
                        b = cnt % 4
                        cnt += 1
                        for j in range(NJ):
                            k.mm(bank(b), blk.v(j * 128, (j + 1) * 128), ACTT.v(j * 1024 + t2 * TT, j * 1024 + (t2 + 1) * TT),
                                 start=(j == 0), stop=(j == NJ - 1))
                        xv = xT.v(dc * S + t * TT, dc * S + (t + 1) * TT)
                        k.tt(xv, xv, bank(b), ALU.add)
                    ws.done(l, "wd%d" % dc)

        for l in range(n_layers):
            try:
                layer_body(l)
            except _Stop:
                break

        for c in range(KC):
            k.dma_out("sp", "yout", yT_d[:, c * S:(c + 1) * S], xT.v(c * S, (c + 1) * S))
        P.emit(nc, block, sems, dsems)
        build_program.stats = P.stats
    return nc


def _kc_layout(W):
    c = W.shape[0] // 128
    n = W.shape[1]
    return np.ascontiguousarray(W.reshape(c, 128, n).transpose(1, 0, 2)).reshape(128, c * n)


def _rope_tables():
    pos = np.arange(S, dtype=np.float32)

    def cs(p, dim):
        inv = (1.0 / (THETA ** (np.arange(0, dim, 2, dtype=np.float32) / np.float32(dim)))).astype(np.float32)
        ang = (p[:, None] * inv[None, :]).astype(np.float32)
        return np.cos(ang).astype(np.float32), np.sin(ang).astype(np.float32)

    tabs = np.zeros((128, 6, S), np.float32)
    c, s_ = cs(pos, 64)
    for r in range(128):
        j = (r % 64) % 32
        tabs[r, 0] = c[:, j]
        tabs[r, 1] = s_[:, j]
    cr, sr = cs((np.arange(S) // 64).astype(np.float32), 32)
    cc, sc = cs((np.arange(S) % 64).astype(np.float32), 32)
    for r in range(128):
        i = r % 64
        j = (i % 32) % 16
        if i < 32:
            tabs[r, 2] = cr[:, j]
            tabs[r, 3] = sr[:, j]
        else:
            tabs[r, 2] = cc[:, j]
            tabs[r, 3] = sc[:, j]
    cm, sm_ = cs(pos, 32)
    tabs[:, 4] = 1.0
    tabs[:, 5] = 0.0
    for i in range(32):
        tabs[64 + i, 4] = cm[:, i % 16]
        tabs[64 + i, 5] = sm_[:, i % 16]
    return tabs.reshape(128, 6 * S)


def _rot_matrix(blocks):
    R = np.zeros((128, 128), np.float32)
    for base, half in blocks:
        for i in range(half):
            R[base + half + i, base + i] = -1.0
            R[base + i, base + half + i] = 1.0
    return R


def _consts():
    import ml_dtypes
    ones = np.ones((128, 128), np.float32)
    bd = np.zeros((128, 128), np.float32)
    bd[0:64, 0:64] = 1.0
    bd[64:128, 64:128] = 1.0
    r_diff = _rot_matrix([(0, 32), (64, 32)])
    r_gqa = _rot_matrix([(0, 16), (32, 16), (64, 16), (96, 16)])
    r_mla = _rot_matrix([(64, 16)])
    sel = np.zeros((128, 128), np.float32)
    for i in range(32):
        sel[i, 64 + i] = 1.0
    ones96 = np.zeros((128, 128), np.float32)
    ones96[0:96, 0:96] = 1.0
    cst = np.concatenate([ones, bd, r_diff, r_gqa, r_mla, sel, ones96], axis=1)
    return cst.astype(ml_dtypes.bfloat16), _rope_tables().astype(ml_dtypes.bfloat16)


def _layer_weights(inp, l):
    w_in = np.asarray(inp["w_in"][l], np.float32)
    cq, ckv, kr = w_in[:, 0:384], w_in[:, 384:640], w_in[:, 640:672]
    dq, dk, dv = w_in[:, 672:1184], w_in[:, 1184:1696], w_in[:, 1696:2208]
    gq, gk, gv = w_in[:, 2208:2592], w_in[:, 2592:2720], w_in[:, 2720:2848]
    w_o = np.asarray(inp["w_o"][l], np.float32)
    w_uq = np.asarray(inp["mla_w_uq"][l], np.float32)
    w_ukv = np.asarray(inp["mla_w_ukv"][l], np.float32)
    w_gu = np.asarray(inp["w_gate_up"][l], np.float32)
    w_dn = np.asarray(inp["w_down"][l], np.float32)
    parts = {}
    for h in range(4):
        sl = slice(h * 128, (h + 1) * 128)
        parts["dw%d" % h] = _kc_layout(np.concatenate([dq[:, sl], dk[:, sl], dv[:, sl]], axis=1))
    wod = w_o[384:896].reshape(4, 128, 1024).transpose(1, 0, 2)
    parts["woda"] = np.ascontiguousarray(wod[:, :, 0:512]).reshape(128, 4 * 512)
    parts["wodb"] = np.ascontiguousarray(wod[:, :, 512:1024]).reshape(128, 4 * 512)
    parts["gwkv"] = _kc_layout(np.concatenate([gk[:, 0:64], gk[:, 0:64], gk[:, 64:128], gk[:, 64:128], gv], axis=1))
    parts["gwq"] = _kc_layout(gq)
    parts["wog"] = _kc_layout(w_o[896:1280])
    parts["mcq"] = _kc_layout(cq)
    parts["mckv"] = _kc_layout(ckv)
    parts["mkr"] = _kc_layout(kr)
    wuqp = np.zeros((384, 6, 128), np.float32)
    wukk = np.zeros((256, 6, 128), np.float32)
    wukv = np.zeros((256, 6, 64), np.float32)
    for h in range(6):
        wuqp[:, h, 0:96] = w_uq[:, h * 96:(h + 1) * 96]
        wukk[:, h, 0:64] = w_ukv[:, h * 128:h * 128 + 64]
        wukv[:, h, :] = w_ukv[:, h * 128 + 64:(h + 1) * 128]
    parts["mlaw"] = np.concatenate([_kc_layout(wuqp.reshape(384, 768)), _kc_layout(wukk.reshape(256, 768)),
                                    _kc_layout(wukv.reshape(256, 384))], axis=1)
    parts["wom"] = _kc_layout(w_o[0:384])
    for j in range(NJ):
        parts["wgu%d" % j] = _kc_layout(np.concatenate([w_gu[:, j * 128:(j + 1) * 128],
                                                         w_gu[:, FF + j * 128:FF + (j + 1) * 128]], axis=1))
    for dc in range(8):
        parts["wd%d" % dc] = _kc_layout(w_dn[:, dc * 128:(dc + 1) * 128])
    W = np.empty((128, TOTW), np.float32)
    for n, c in BLOCKS:
        o, cc = BOFF[n]
        assert parts[n].shape == (128, cc), (n, parts[n].shape, cc)
        W[:, o:o + cc] = parts[n]
    return W


def _layer_vecs(inp, l):
    V = np.zeros((128, NV), np.float32)
    g = lambda name: np.asarray(inp[name][l], np.float32)
    V[:, 0:8] = g("attn_norm").reshape(8, 128).T
    V[:, 8:16] = g("ffn_norm").reshape(8, 128).T
    V[:, 16:19] = g("mla_q_norm").reshape(3, 128).T
    V[:, 19:21] = g("mla_kv_norm").reshape(2, 128).T
    V[0:96, 21] = g("mla_q_gain")
    V[0:96, 22] = g("mla_k_gain")
    V[:, 23] = np.tile(g("diff_q_gain"), 2)
    V[:, 24] = np.tile(g("diff_k_gain"), 2)
    V[:, 25] = g("diff_out_gain")
    V[:, 26] = np.tile(g("gqa_q_gain"), 2)
    V[:, 27] = np.tile(g("gqa_k_gain"), 2)
    V[:, 28:92] = g("diff_lq1")[None, :]
    V[:, 92:156] = g("diff_lk1")[None, :]
    V[:, 156:220] = g("diff_lq2")[None, :]
    V[:, 220:284] = g("diff_lk2")[None, :]
    return V


def _x_to_dev(xb):
    return np.ascontiguousarray(xb.T.reshape(KC, 128, S).transpose(1, 0, 2)).reshape(128, KC * S)


def _x_from_dev(yT):
    return np.ascontiguousarray(yT.reshape(128, KC, S).transpose(2, 1, 0)).reshape(S, D)


_PROG_CACHE = {}


def kernel(**inputs):
    x = np.asarray(inputs["x"], np.float32)
    B = x.shape[0]
    cst, tab = _consts()
    Ws = [_layer_weights(inputs, l) for l in range(DEPTH)]
    Vs = [_layer_vecs(inputs, l) for l in range(DEPTH)]
    if "nc" not in _PROG_CACHE:
        _PROG_CACHE["nc"] = build_program(DEPTH)
    nc = _PROG_CACHE["nc"]
    in_maps = []
    for b in range(B):
        m = {"xT": _x_to_dev(x[b]), "tab": tab, "cst": cst}
        for l in range(DEPTH):
            m["w%d" % l] = Ws[l]
            m["v%d" % l] = Vs[l]
        in_maps.append(m)
    res = run_bass_kernel_spmd(nc, in_maps, core_ids=list(range(B)))
    out = np.stack([_x_from_dev(np.asarray(r["yT"], np.float32)) for r in res.results], axis=0)
    return out.astype(np.float32)
```
